# Optimizing a Trainium2 kernel written in Bass

```python
import math
import jax, jax.numpy as jnp
from jax import lax
import numpy as np

D_MODEL = 1024
BATCH = 1
SEQ = 16384
DEPTH = 1
DEC_BATCH = 128
DEC_SEQ = 1
PAST_LEN = 8192
PAGE_SIZE = 128

HEAD_DIM = 64
HEADS_PER_GROUP = 8
ATTN_GROUPS = ((128, 1), (512, 4), (2048, 16))
N_ATTN_GROUPS = len(ATTN_GROUPS)
ATTN_QKV_W = N_ATTN_GROUPS * HEADS_PER_GROUP * HEAD_DIM
ATTN_OUT_W = HEADS_PER_GROUP * HEAD_DIM
ROT_DIM = HEAD_DIM // 4
ROPE_THETA = 500000.0
Q_BLOCK = 128
SSM_WIDTH = D_MODEL // 2
SSM_GROUP = 16
SSM_GROUPS = SSM_WIDTH // SSM_GROUP
SSM_STATE = 64
DT_MIN = 1e-3
DT_MAX = 1e-1
D_FF = 4 * D_MODEL
EPS = 1e-6
IN_W = 3 * ATTN_QKV_W + SSM_WIDTH + 2 * D_MODEL
SPLITS = (ATTN_QKV_W, 2 * ATTN_QKV_W, 3 * ATTN_QKV_W, 3 * ATTN_QKV_W + SSM_WIDTH,
          3 * ATTN_QKV_W + SSM_WIDTH + D_MODEL)

kernel_name = "hybrid_s5_dilated_attn_decode_step"


def rmsnorm(x, g):
    x32 = x.astype(jnp.float32)
    y = x32 * lax.rsqrt(jnp.mean(x32 * x32, axis=-1, keepdims=True) + EPS)
    return (y * g.astype(jnp.float32)).astype(x.dtype)


def rotary(x, pos):
    half = ROT_DIM // 2
    inv_freq = jnp.exp(-math.log(ROPE_THETA) * jnp.arange(half, dtype=jnp.float32) * (2.0 / ROT_DIM))
    ang = pos.astype(jnp.float32)[:, None] * inv_freq[None, :]
    bshape = (1, pos.shape[0]) + (1,) * (x.ndim - 3) + (half,)
    cos = jnp.cos(ang).reshape(bshape)
    sin = jnp.sin(ang).reshape(bshape)
    x1 = x[..., :half].astype(jnp.float32)
    x2 = x[..., half:ROT_DIM].astype(jnp.float32)
    rot = jnp.concatenate([x1 * cos - x2 * sin, x2 * cos + x1 * sin], axis=-1).astype(x.dtype)
    return jnp.concatenate([rot, x[..., ROT_DIM:]], axis=-1)


def dilated_attn_prompt(q, k, v, window, dil):
    b, s, h, e = q.shape
    n = s // dil
    wk = window // dil
    bq = math.gcd(n, Q_BLOCK)
    nb = n // bq

    def sub(t):
        return t.reshape(b, n, dil, h, e).transpose(0, 2, 1, 3, 4)

    qs = sub(q).reshape(b, dil, nb, bq, h, e)
    pad = ((0, 0), (0, 0), (wk, 0), (0, 0), (0, 0))
    kp = jnp.pad(sub(k), pad)
    vp = jnp.pad(sub(v), pad)
    idx = jnp.arange(nb)[:, None] * bq + jnp.arange(bq + wk)[None, :]
    kw = kp[:, :, idx]
    vw = vp[:, :, idx]
    scores = jnp.einsum('brnqhe,brnkhe->brnhqk', qs, kw,
                        preferred_element_type=jnp.float32) * (HEAD_DIM ** -0.5)
    qi = jnp.arange(bq)[:, None]
    kj = jnp.arange(bq + wk)[None, :]
    dist = qi + wk - kj
    kidx = (jnp.arange(nb) * bq)[:, None, None] + kj[None] - wk
    mask = ((dist >= 0) & (dist <= wk))[None] & (kidx >= 0)
    scores = jnp.where(mask[None, None, :, None], scores, -jnp.inf)
    m = jnp.max(scores, axis=-1, keepdims=True)
    p = jnp.exp(scores - m)
    l = jnp.sum(p, axis=-1, keepdims=True)
    o = jnp.einsum('brnhqk,brnkhe->brnhqe', p / l, vw.astype(jnp.float32))
    lse = (m + jnp.log(l))[..., 0]
    o = o.transpose(0, 1, 2, 4, 3, 5).reshape(b, dil, n, h, e).transpose(0, 2, 1, 3, 4).reshape(b, s, h, e)
    lse = lse.transpose(0, 1, 2, 4, 3).reshape(b, dil, n, h).transpose(0, 2, 1, 3).reshape(b, s, h)
    return o, lse


def dilated_attn_sample(q, k, v, kv_cache, window, dil):
    bd, t, h, e = q.shape
    lc = kv_cache.shape[1]
    kv_all = jnp.concatenate([kv_cache, jnp.stack([k, v], axis=2).astype(kv_cache.dtype)], axis=1)
    nk = window // dil + 1
    idx = lc + jnp.arange(t)[:, None] - dil * jnp.arange(nk)[None, :]
    valid = idx >= 0
    g = kv_all[:, jnp.maximum(idx, 0)]
    scores = jnp.einsum('bthe,btkhe->bthk', q, g[:, :, :, 0],
                        preferred_element_type=jnp.float32) * (HEAD_DIM ** -0.5)
    scores = jnp.where(valid[None, :, None, :], scores, -jnp.inf)
    m = jnp.max(scores, axis=-1, keepdims=True)
    p = jnp.exp(scores - m)
    l = jnp.sum(p, axis=-1, keepdims=True)
    o = jnp.einsum('bthk,btkhe->bthe', p / l, g[:, :, :, 1].astype(jnp.float32))
    lse = (m + jnp.log(l))[..., 0]
    return o, lse, kv_all[:, t:]


def s5_branch(u, h0, lam_re, lam_im, log_dt, b_re, b_im, c_re, c_im, d_skip, w_glu, b_glu):
    f32 = jnp.float32
    b, s = u.shape[:2]
    u32 = u.astype(f32)
    lam = lax.complex(lam_re.astype(f32), lam_im.astype(f32))
    dt = jnp.exp(log_dt.astype(f32))[:, None]
    lam_bar = jnp.exp(lam * dt)
    b_bar = ((lam_bar - 1.0) / lam)[..., None] * lax.complex(b_re.astype(f32), b_im.astype(f32))
    ug = u32.reshape(b, s, SSM_GROUPS, SSM_GROUP)
    bu = jnp.einsum('bsgp,gnp->bsgn', ug, b_bar)
    if h0 is not None:
        bu = bu.at[:, 0].add(lam_bar * h0)
    a = jnp.broadcast_to(lam_bar, bu.shape)

    def combine(e1, e2):
        a1, x1 = e1
        a2, x2 = e2
        return a1 * a2, a2 * x1 + x2

    _, hs = lax.associative_scan(combine, (a, bu), axis=1)
    c = lax.complex(c_re.astype(f32), c_im.astype(f32))
    y = jnp.einsum('bsgn,gpn->bsgp', hs, c).real.reshape(b, s, SSM_WIDTH) + d_skip.astype(f32) * u32
    y = jax.nn.gelu(y)
    y = y * jax.nn.sigmoid(y @ w_glu.astype(f32) + b_glu.astype(f32))
    return y.astype(u.dtype), hs[:, -1]


def decoder_layer(x, pos, kv_caches, h0, lw):
    (norm_mix, w_in, lam_re, lam_im, log_dt, b_re, b_im, c_re, c_im, d_skip, w_glu, b_glu,
     w_branch_ssm, w_branch_attn, w_out, norm_mlp, w_up, w_down) = lw
    b, s = x.shape[:2]
    h = rmsnorm(x, norm_mix)
    q, k, v, u, g_ssm, g_attn = jnp.split(h @ w_in, SPLITS, axis=-1)
    shp = (b, s, N_ATTN_GROUPS, HEADS_PER_GROUP, HEAD_DIM)
    q = rotary(q.reshape(shp), pos)
    k = rotary(k.reshape(shp), pos)
    v = v.reshape(shp)
    outs, lses, new_kv = [], [], []
    for gi, (window, dil) in enumerate(ATTN_GROUPS):
        if kv_caches is None:
            o, lse = dilated_attn_prompt(q[:, :, gi], k[:, :, gi], v[:, :, gi], window, dil)
            nkv = jnp.stack([k[:, :, gi], v[:, :, gi]], axis=2)[:, s - min(window, s):]
        else:
            o, lse, nkv = dilated_attn_sample(q[:, :, gi], k[:, :, gi], v[:, :, gi], kv_caches[gi], window, dil)
        outs.append(o)
        lses.append(lse)
        new_kv.append(nkv)
    alpha = jax.nn.softmax(jnp.stack(lses), axis=0)
    attn = jnp.einsum('gbsh,gbshe->bshe', alpha, jnp.stack(outs)).reshape(b, s, ATTN_OUT_W).astype(x.dtype)
    ssm_y, h_last = s5_branch(u, h0, lam_re, lam_im, log_dt, b_re, b_im, c_re, c_im, d_skip, w_glu, b_glu)
    z = jax.nn.sigmoid(g_ssm) * (ssm_y @ w_branch_ssm) + jax.nn.sigmoid(g_attn) * (attn @ w_branch_attn)
    x = x + z @ w_out
    hm = rmsnorm(x, norm_mlp)
    x = x + jnp.square(jax.nn.relu(hm @ w_up)) @ w_down
    return x, new_kv, h_last


def setup_inputs(seed: int = 0) -> dict:
    key = jax.random.key(seed)
    ks = jax.random.split(key, 32)
    f32 = jnp.float32

    def nrm(k, shape, scale):
        return jax.random.normal(k, shape, f32) * scale

    G, N = SSM_GROUPS, SSM_STATE
    lam_re = -0.5 + nrm(ks[6], (DEPTH, G, N), 0.01)
    lam_im = math.pi * jnp.arange(N, dtype=f32)[None, None, :] + nrm(ks[7], (DEPTH, G, N), 0.01)
    log_dt = jax.random.uniform(ks[8], (DEPTH, G), f32, math.log(DT_MIN), math.log(DT_MAX))
    return {
        "x_prompt": nrm(ks[0], (BATCH, SEQ, D_MODEL), 1.0),
        "x_sample": nrm(ks[1], (DEC_BATCH, DEC_SEQ, D_MODEL), 1.0),
        "cache_kv_g1": nrm(ks[2], (DEPTH, DEC_BATCH, min(ATTN_GROUPS[0][0], PAST_LEN), 2, HEADS_PER_GROUP, HEAD_DIM), 1.0),
        "cache_kv_g2": nrm(ks[3], (DEPTH, DEC_BATCH, min(ATTN_GROUPS[1][0], PAST_LEN), 2, HEADS_PER_GROUP, HEAD_DIM), 1.0),
        "cache_kv_g3": nrm(ks[4], (DEPTH, DEC_BATCH, min(ATTN_GROUPS[2][0], PAST_LEN), 2, HEADS_PER_GROUP, HEAD_DIM), 1.0),
        "state_ssm": nrm(ks[5], (DEPTH, DEC_BATCH, G, N, 2), 0.3),
        "norm_mix": 1.0 + nrm(ks[9], (DEPTH, D_MODEL), 0.02),
        "w_in": nrm(ks[10], (DEPTH, D_MODEL, IN_W), D_MODEL ** -0.5),
        "ssm_lambda_re": lam_re,
        "ssm_lambda_im": lam_im,
        "ssm_log_dt": log_dt,
        "ssm_b_re": nrm(ks[11], (DEPTH, G, N, SSM_GROUP), SSM_GROUP ** -0.5),
        "ssm_b_im": nrm(ks[12], (DEPTH, G, N, SSM_GROUP), SSM_GROUP ** -0.5),
        "ssm_c_re": nrm(ks[13], (DEPTH, G, SSM_GROUP, N), N ** -0.5),
        "ssm_c_im": nrm(ks[14], (DEPTH, G, SSM_GROUP, N), N ** -0.5),
        "ssm_d": nrm(ks[15], (DEPTH, SSM_WIDTH), 1.0),
        "w_glu": nrm(ks[16], (DEPTH, SSM_WIDTH, SSM_WIDTH), SSM_WIDTH ** -0.5),
        "b_glu": nrm(ks[17], (DEPTH, SSM_WIDTH), 0.01),
        "w_branch_ssm": nrm(ks[18], (DEPTH, SSM_WIDTH, D_MODEL), SSM_WIDTH ** -0.5),
        "w_branch_attn": nrm(ks[19], (DEPTH, ATTN_OUT_W, D_MODEL), ATTN_OUT_W ** -0.5),
        "w_out": nrm(ks[20], (DEPTH, D_MODEL, D_MODEL), D_MODEL ** -0.5),
        "norm_mlp": 1.0 + nrm(ks[21], (DEPTH, D_MODEL), 0.02),
        "w_up": nrm(ks[22], (DEPTH, D_MODEL, D_FF), D_MODEL ** -0.5),
        "w_down": nrm(ks[23], (DEPTH, D_FF, D_MODEL), D_FF ** -0.5),
        "norm_final": 1.0 + nrm(ks[24], (D_MODEL,), 0.02),
    }


def reference(x_prompt, x_sample, cache_kv_g1, cache_kv_g2, cache_kv_g3, state_ssm,
              norm_mix, w_in, ssm_lambda_re, ssm_lambda_im, ssm_log_dt, ssm_b_re, ssm_b_im,
              ssm_c_re, ssm_c_im, ssm_d, w_glu, b_glu, w_branch_ssm, w_branch_attn, w_out,
              norm_mlp, w_up, w_down, norm_final):
    pos_p = jnp.arange(x_prompt.shape[1])
    pos_s = PAST_LEN + jnp.arange(x_sample.shape[1])
    yp, ys = x_prompt, x_sample
    kvp = ([], [], [])
    kvs = ([], [], [])
    hp_list, hs_list = [], []
    for l in range(DEPTH):
        lw = (norm_mix[l], w_in[l], ssm_lambda_re[l], ssm_lambda_im[l], ssm_log_dt[l], ssm_b_re[l],
              ssm_b_im[l], ssm_c_re[l], ssm_c_im[l], ssm_d[l], w_glu[l], b_glu[l], w_branch_ssm[l],
              w_branch_attn[l], w_out[l], norm_mlp[l], w_up[l], w_down[l])
        yp, nkv_p, h_p = decoder_layer(yp, pos_p, None, None, lw)
        st = state_ssm[l]
        h0 = lax.complex(st[..., 0].astype(jnp.float32), st[..., 1].astype(jnp.float32))
        ys, nkv_s, h_s = decoder_layer(ys, pos_s, (cache_kv_g1[l], cache_kv_g2[l], cache_kv_g3[l]), h0, lw)
        for gi in range(N_ATTN_GROUPS):
            kvp[gi].append(nkv_p[gi])
            kvs[gi].append(nkv_s[gi])
        hp_list.append(jnp.stack([h_p.real, h_p.imag], axis=-1).astype(state_ssm.dtype))
        hs_list.append(jnp.stack([h_s.real, h_s.imag], axis=-1).astype(state_ssm.dtype))
    y_prompt = rmsnorm(yp, norm_final)
    y_sample = rmsnorm(ys, norm_final)
    kv_g1_prompt = jnp.stack(kvp[0], axis=0)
    kv_g2_prompt = jnp.stack(kvp[1], axis=0)
    kv_g3_prompt = jnp.stack(kvp[2], axis=0)
    ssm_prompt = jnp.stack(hp_list, axis=0)
    kv_g1_sample = jnp.stack(kvs[0], axis=0)
    kv_g2_sample = jnp.stack(kvs[1], axis=0)
    kv_g3_sample = jnp.stack(kvs[2], axis=0)
    ssm_sample = jnp.stack(hs_list, axis=0)
    return (y_prompt, y_sample, kv_g1_prompt, kv_g2_prompt, kv_g3_prompt, ssm_prompt,
            kv_g1_sample, kv_g2_sample, kv_g3_sample, ssm_sample)
```

```python
import math
import numpy as np
import concourse.bass as bass
import concourse.mybir as mybir
from concourse.bass_utils import run_bass_kernel_spmd

F32 = mybir.dt.float32
BF16 = mybir.dt.bfloat16
U8 = mybir.dt.uint8
ALU = mybir.AluOpType
AF = mybir.ActivationFunctionType
AX = mybir.AxisListType

import os
DBG_STAGE = int(os.environ.get('KSTAGE', '9'))
DBG_NCH = int(os.environ.get('KNCH', '28'))
DBG_FE = int(os.environ.get('KFE', '9'))
DBG_NT = int(os.environ.get('KNT', '8'))
DBG_SKIP = int(os.environ.get('KSKIP', '0'))
NCORE = 8
TOWN = 2048
TEXT = 4096
NDEC = 16
TALL = TEXT + NDEC
TPRI = 14336
PI = math.pi
NEG = -30000.0
COMPUTE = ("tensor", "vector", "scalar", "gpsimd")
SAME_ENGINE_SYNC = ("vector", "scalar", "gpsimd")
DMA_SLOTS = {"sync": 16, "gpsimd": 8, "scalar": 56}


class Prog:
    def __init__(self, nc):
        self.nc = nc
        self.ops = []
        self.bar = 0

    def op(self, eng, fn, reads=(), writes=()):
        self.ops.append(dict(eng=eng, fn=fn, reads=tuple(reads), writes=tuple(writes), dma=False, bar=self.bar))

    def dma(self, queue, fn, reads=(), writes=()):
        self.ops.append(dict(eng=queue, fn=fn, reads=tuple(reads), writes=tuple(writes), dma=True, bar=self.bar))

    def barrier(self):
        self.bar += 1

    def mm(self, out, lhsT, rhs, start=True, stop=True, reads=(), writes=()):
        self.op("tensor", lambda e: e.matmul(out, lhsT=lhsT, rhs=rhs, start=start, stop=stop, skip_group_check=True), reads, writes)

    def tr(self, out, in_, ident, reads=(), writes=()):
        self.op("tensor", lambda e: e.transpose(out, in_, ident), reads, writes)

    def act(self, out, in_, func, reads=(), writes=(), **kw):
        self.op("scalar", lambda e: e.activation(out=out, in_=in_, func=func, **kw), reads, writes)

    def tt(self, eng, out, in0, in1, op, reads=(), writes=()):
        self.op(eng, lambda e: e.tensor_tensor(out=out, in0=in0, in1=in1, op=op), reads, writes)

    def ts(self, eng, out, in0, s1, s2, op0, op1=None, reads=(), writes=()):
        if op1 is None:
            self.op(eng, lambda e: e.tensor_scalar(out=out, in0=in0, scalar1=s1, scalar2=s2, op0=op0), reads, writes)
        else:
            self.op(eng, lambda e: e.tensor_scalar(out=out, in0=in0, scalar1=s1, scalar2=s2, op0=op0, op1=op1), reads, writes)

    def stt(self, eng, out, in0, scalar, in1, op0, op1, reads=(), writes=()):
        self.op(eng, lambda e: e.scalar_tensor_tensor(out=out, in0=in0, scalar=scalar, in1=in1, op0=op0, op1=op1), reads, writes)

    def cp(self, eng, out, in_, reads=(), writes=()):
        if eng == "scalar":
            self.op(eng, lambda e: e.activation(out=out, in_=in_, func=AF.Copy), reads, writes)
        else:
            self.op(eng, lambda e: e.tensor_copy(out=out, in_=in_), reads, writes)

    def memset(self, eng, ap, val, writes=()):
        self.op(eng, lambda e: e.memset(ap, val), (), writes)

    def scan(self, out, d0, d1, initial, reads=(), writes=()):
        self.op("vector", lambda e: e.tensor_tensor_scan(out=out, data0=d0, data1=d1, initial=initial, op0=ALU.mult, op1=ALU.add), reads, writes)

    def red(self, eng, out, in_, op, axis, reads=(), writes=()):
        self.op(eng, lambda e: e.tensor_reduce(out=out, in_=in_, axis=axis, op=op), reads, writes)

    def recip(self, out, in_, reads=(), writes=()):
        self.op("vector", lambda e: e.reciprocal(out=out, in_=in_), reads, writes)

    def ld(self, q, out, in_, reads=(), writes=()):
        self.dma(q, lambda e: e.dma_start(out=out, in_=in_), reads, writes)

    def build(self):
        nc = self.nc
        ops = self.ops
        n = len(ops)
        last_writer = {}
        readers = {}
        deps = [set() for _ in range(n)]
        for i, o in enumerate(ops):
            for r in o["reads"]:
                w = last_writer.get(r)
                if w is not None:
                    deps[i].add(w)
            for r in o["writes"]:
                w = last_writer.get(r)
                if w is not None:
                    deps[i].add(w)
                for rd in readers.get(r, ()):
                    if rd != i:
                        deps[i].add(rd)
            for r in o["reads"]:
                readers.setdefault(r, []).append(i)
            for r in o["writes"]:
                last_writer[r] = i
                readers[r] = []
        nb = self.bar
        if nb:
            last_of = {}
            dmas_before = []
            seen_first = set()
            cur = 0
            snapshot = None
            for i, o in enumerate(ops):
                if o["bar"] != cur:
                    cur = o["bar"]
                    snapshot = (dict(last_of), list(dmas_before))
                    seen_first = set()
                if snapshot is not None and o["eng"] not in seen_first:
                    seen_first.add(o["eng"])
                    deps[i].update(snapshot[0].values())
                    deps[i].update(snapshot[1])
                if o["dma"]:
                    dmas_before.append(i)
                else:
                    last_of[o["eng"]] = i
        for i, o in enumerate(ops):
            keep = set()
            best = {}
            for d in deps[i]:
                od = ops[d]
                if od["dma"]:
                    keep.add(d)
                    continue
                if od["eng"] == o["eng"] and od["eng"] not in SAME_ENGINE_SYNC:
                    continue
                e = od["eng"]
                if e not in best or d > best[e]:
                    best[e] = d
            keep.update(best.values())
            deps[i] = keep
        needed = set()
        for i in range(n):
            needed.update(deps[i])
        tick = {}
        eng_count = {e: 0 for e in COMPUTE}
        dma_queues = sorted({o["eng"] for o in ops if o["dma"]})
        dma_count = {q: 0 for q in dma_queues}
        dma_slot = {}
        for i, o in enumerate(ops):
            if o["dma"]:
                q = o["eng"]
                k = dma_count[q]
                dma_count[q] += 1
                ns = DMA_SLOTS[q]
                dma_slot[i] = (q, k % ns, k // ns + 1)
            elif i in needed:
                eng_count[o["eng"]] += 1
                tick[i] = eng_count[o["eng"]]
        sems = {e: nc.alloc_semaphore("s_" + e) for e in COMPUTE}
        dsems = {}
        for q in dma_queues:
            for s in range(min(DMA_SLOTS[q], dma_count[q])):
                dsems[(q, s)] = nc.alloc_semaphore("d_%s_%d" % (q, s))
        by_eng = {}
        for i, o in enumerate(ops):
            by_eng.setdefault(o["eng"], []).append(i)
        self.stats = {e: len(v) for e, v in by_eng.items()}

        def emit(engname, eng):
            waited = {}

            def wait(key, sem, val):
                if waited.get(key, 0) >= val:
                    return
                waited[key] = val
                eng.wait_ge(sem, val)

            for i in by_eng.get(engname, []):
                o = ops[i]
                for d in sorted(deps[i]):
                    od = ops[d]
                    if od["dma"]:
                        q, slot, use = dma_slot[d]
                        wait(("d", q, slot), dsems[(q, slot)], 16 * use)
                    else:
                        wait(("e", od["eng"]), sems[od["eng"]], tick[d])
                if o["dma"]:
                    q, slot, use = dma_slot[i]
                    if use > 1:
                        wait(("d", q, slot), dsems[(q, slot)], 16 * (use - 1))
                    o["fn"](eng).then_inc(dsems[(q, slot)], 16)
                else:
                    ins = o["fn"](eng)
                    if i in tick:
                        ins.then_inc(sems[o["eng"]], 1)
            if engname == "sync":
                last = {}
                for i in sorted(dma_slot):
                    q, slot, use = dma_slot[i]
                    last[(q, slot)] = use
                for (q, slot), use in last.items():
                    wait(("d", q, slot), dsems[(q, slot)], 16 * use)

        with nc.Block() as block:
            @block.sync
            def _(e):
                emit("sync", e)

            @block.tensor
            def _(e):
                emit("tensor", e)

            @block.vector
            def _(e):
                emit("vector", e)

            @block.scalar
            def _(e):
                emit("scalar", e)

            @block.gpsimd
            def _(e):
                emit("gpsimd", e)


class Arena:
    def __init__(self, nc, nbytes):
        self.t = nc.alloc_sbuf_tensor("arena", [128, nbytes], U8)
        self.nbytes = nbytes
        self.top = 0

    def alloc(self, shape, dtype):
        esz = 4 if dtype == F32 else 2
        n = 1
        for s in shape:
            n *= s
        nb = (n * esz + 31) // 32 * 32
        assert self.top + nb <= self.nbytes, "arena overflow %d + %d > %d" % (self.top, nb, self.nbytes)
        ap = self.t[:, self.top:self.top + n * esz].bitcast(dtype)
        self.top += nb
        if len(shape) == 2:
            ap = ap.rearrange("p (a b) -> p a b", a=shape[0])
        elif len(shape) == 3:
            ap = ap.rearrange("p (a b c) -> p a b c", a=shape[0], b=shape[1])
        return ap

    def mark(self):
        return self.top

    def release(self, m):
        self.top = m


def _tile_index():
    TI = {}
    cols = []

    def add(name, c):
        TI[name] = len(cols)
        cols.append(np.asarray(c, np.int64))

    j = np.arange(128)
    e = j % 64
    partner = np.where(e < 8, j + 8, np.where(e < 16, j - 8, j))
    for g in range(3):
        for hp in range(4):
            base = g * 512 + hp * 128
            add(("q", g, hp), base + j)
            add(("qp", g, hp), base + partner)
            add(("k", g, hp), 1536 + base + j)
            add(("kp", g, hp), 1536 + base + partner)
            add(("v", g, hp), 3072 + base + j)
    for ct in range(4):
        add(("u", ct), 4608 + ct * 128 + j)
    for d in range(8):
        add(("gs", d), 5120 + d * 128 + j)
        add(("ga", d), 6144 + d * 128 + j)
    return TI, cols


TI, TCOLS = _tile_index()
NT = len(TCOLS)
KSTART = (1920, 1536, 0)


def build_program(phases="ABCDEFG", dbg=False):
    nc = bass.Bass("TRN2", target_bir_lowering=False)
    P = Prog(nc)

    def din(name, shape):
        return nc.dram_tensor(name, list(shape), F32, kind="ExternalInput").ap()

    def dout(name, shape):
        return nc.dram_tensor(name, list(shape), F32, kind="ExternalOutput").ap()

    x_all = din("x_all", [TPRI + TOWN, 1024])
    x_dec = din("x_dec", [NDEC, 1024])
    w_tiles = din("w_tiles", [NT, 128, 8, 128])
    w_kv = din("w_kv", [6, 128, 8, 512])
    w_glu = din("w_glu", [128, 4, 512])
    w_bs = din("w_bs", [128, 4, 1024])
    w_ba = din("w_ba", [128, 4, 1024])
    w_out = din("w_out", [128, 8, 1024])
    w_up = din("w_up", [128, 8, 4096])
    w_down = din("w_down", [128, 32, 1024])
    gm_d = din("gm", [128, 8])
    gmlp_d = din("gmlp", [128, 8])
    gfin_d = din("gfin", [128, 1024])
    ssm_small = din("ssm_small", [128, 16, 3])
    b_re_d = din("b_re", [128, 16, 16])
    b_im_d = din("b_im", [128, 16, 16])
    wc_re_d = din("wc_re", [128, 16, 128])
    wc_im_d = din("wc_im", [128, 16, 128])
    dskip_d = din("dskip", [128, 4])
    bglu_d = din("bglu", [128, 4])
    h0_d = din("h0", [128, 16, 16, 2])
    rope_c_d = din("rope_c", [128, TALL])
    rope_s_d = din("rope_s", [128, TALL])
    rope_tm_d = din("rope_tm", [128, 17, 16])
    mask2_d = din("mask2", [128, 256])
    maskh_d = din("maskh", [128, 128])
    em_d = din("em", [8, 512])
    bones_d = din("bones", [128, 128])
    caches = [din("cache1", [NDEC, 128, 2, 512]), din("cache2", [NDEC, 512, 2, 512]), din("cache3", [NDEC, 2048, 2, 512])] if ("G" in phases or "F" in phases) else None
    LC = (128, 512, 2048)
    DIL = (1, 4, 16)

    y_o = dout("y", [TOWN, 1024])
    yd_o = dout("y_dec", [NDEC, 1024])
    kvp = [dout("kvp1", [128, 2, 512]), dout("kvp2", [512, 2, 512]), dout("kvp3", [2048, 2, 512])]
    ssmp_o = dout("ssmp", [128, 16, 2])
    kvs = [dout("kvs1", [NDEC, 128, 2, 512]), dout("kvs2", [NDEC, 512, 2, 512]), dout("kvs3", [NDEC, 2048, 2, 512])] if ("G" in phases or "E" in phases) else None
    ssms_o = dout("ssms", [128, 16, NDEC, 2])
    x1_scr = nc.dram_tensor("x1_scr", [TOWN + NDEC, 1024], F32).ap()
    dbg_o = dout("dbg", [128, 4096]) if dbg else None

    pacc = nc.alloc_psum_tensor("pacc", [128, 2048], F32)
    pws = [nc.alloc_psum_tensor("pw%d" % i, [128, 512], F32) for i in range(3)]
    ptr_t = nc.alloc_psum_tensor("ptr", [128, 1024], BF16)
    banks = [(pacc[:, 512 * k:512 * (k + 1)], "pacc%d" % k) for k in range(4)] + [(pws[i][:, :], "pw%d" % i) for i in range(3)]
    gen_banks = banks[4:]
    rr = {"i": 0}

    def nextbank(pool=None):
        pool = pool or gen_banks
        b = pool[rr["i"] % len(pool)]
        rr["i"] += 1
        return b

    A = Arena(nc, 204 * 1024)
    ident16 = A.alloc([128], BF16)
    ident32 = A.alloc([128], F32)
    zeros16 = A.alloc([128], BF16)
    ones32 = A.alloc([1], F32)
    picol = A.alloc([1], F32)
    epscol = A.alloc([1], F32)
    gm = A.alloc([8], F32)
    gmlp = A.alloc([8], F32)
    stat = A.alloc([8], F32)
    P.memset("gpsimd", ident16, 1.0, writes=["ident16"])
    P.op("gpsimd", lambda e: e.affine_select(out=ident16, in_=ident16, pattern=[[-1, 128]], compare_op=ALU.is_equal,
                                             fill=0.0, base=0, channel_multiplier=1), ["ident16"], ["ident16"])
    P.cp("gpsimd", ident32, ident16, ["ident16"], ["ident32"])
    P.memset("gpsimd", zeros16, 0.0, writes=["zeros16"])
    P.memset("gpsimd", ones32, 1.0, writes=["ones32"])
    P.memset("gpsimd", picol, PI, writes=["picol"])
    P.memset("gpsimd", epscol, 1e-6, writes=["epscol"])
    P.ld("sync", gm, gm_d[:, :], writes=["gm"])
    P.ld("sync", gmlp, gmlp_d[:, :], writes=["gmlp"])

    th = A.alloc([16], F32)
    are = A.alloc([16], F32)
    r1 = A.alloc([16], F32)
    gi_re = A.alloc([16], F32)
    gi_im = A.alloc([16], F32)
    lb_re = A.alloc([16], F32)
    lb_im = A.alloc([16], F32)
    WB = A.alloc([2, 16, 128], BF16)
    WC = A.alloc([2, 16, 128], BF16)
    tmpc = A.alloc([8, 16], F32)
    NPW = 11
    PWr = A.alloc([NPW, 16], F32)
    PWi = A.alloc([NPW, 16], F32)
    base_mark = A.mark()

    if "G" in phases:
        for g in range(3):
            lc = LC[g]
            for b in range(NDEC):
                P.ld("scalar", kvs[g][b, 0:lc - 1, :, :], caches[g][b, 1:lc, :, :], writes=[("kvs", g, b)])

    xb = [None, None]
    h16b = [None, None]
    junk = [None]
    fe_cnt = {"i": 0}

    def frontend_alloc():
        xb[0] = A.alloc([1024], F32)
        xb[1] = A.alloc([1024], F32)
        h16b[0] = A.alloc([1024], BF16)
        h16b[1] = A.alloc([1024], BF16)
        junk[0] = A.alloc([1024], BF16)

    def frontend(src_rows, ntok, dstT, dst_key, c0, load_q="sync", keep_x=None):
        i = fe_cnt["i"]
        fe_cnt["i"] += 1
        par = i % 2
        x = keep_x if keep_x is not None else xb[par]
        xk = ("xk", id(keep_x)) if keep_x is not None else ("xb", par)
        h16 = h16b[par]
        sc = stat[:, 2 * par:2 * par + 2]
        sk = ("stat", par)
        P.ld(load_q, x[0:ntok, :], src_rows, reads=["x1_scr"] if keep_x is not None else [], writes=[xk])
        P.memset("gpsimd", sc[0:ntok, 0:1], 0.0, writes=[sk])
        P.act(junk[0][0:ntok, :], x[0:ntok, :], AF.Square, reads=[xk, sk], writes=["junk", sk], scale=1.0 / 32.0, accum_out=sc[0:ntok, 0:1])
        if DBG_FE < 2:
            return sc
        P.act(sc[0:ntok, 1:2], sc[0:ntok, 0:1], AF.Sqrt, reads=[sk, "epscol"], writes=[sk], bias=epscol[0:ntok, 0:1])
        P.recip(sc[0:ntok, 1:2], sc[0:ntok, 1:2], reads=[sk], writes=[sk])
        P.ts("vector", h16[0:ntok, :], x[0:ntok, :], sc[0:ntok, 1:2], None, ALU.mult, reads=[xk, sk], writes=[("h16", par)])
        if DBG_FE < 3:
            return sc
        for kt in range(8):
            P.tr(ptr_t[:, kt * 128:kt * 128 + ntok], h16[0:ntok, kt * 128:(kt + 1) * 128], ident16[0:ntok, 0:ntok],
                 reads=[("h16", par), "ident16"], writes=["ptr"])
        if DBG_FE < 4:
            return sc
        P.cp("scalar", dstT[:, :, c0:c0 + ntok], ptr_t[:, :].rearrange("p (a b) -> p a b", a=8)[:, :, 0:ntok], reads=["ptr"], writes=[dst_key])
        return sc

    wst = [None, None]
    w16 = [None, None]
    wt_cnt = {"i": 0}

    def wtile_alloc():
        for i in range(2):
            wst[i] = A.alloc([8, 128], F32)
            w16[i] = A.alloc([8, 128], BF16)

    def load_wtile(idx):
        i = wt_cnt["i"]
        wt_cnt["i"] += 1
        par = i % 2
        P.ld("sync", wst[par], w_tiles[idx, :, :, :], writes=[("wst", par)])
        P.tt("gpsimd", w16[par], wst[par], gm.rearrange("p (a o) -> p a o", o=1).to_broadcast([128, 8, 128]), ALU.mult,
             reads=[("wst", par), "gm"], writes=[("w16", par)])
        return w16[par], ("w16", par)

    def proj(wt, wkey, srcT, skey, c0, n, bank, bkey):
        for kt in range(8):
            P.mm(bank[:, 0:n], wt[:, kt, :], srcT[:, kt, c0:c0 + n], start=(kt == 0), stop=(kt == 7), reads=[wkey, skey], writes=[bkey])

    def chunks(c0, c1, step=512):
        out = []
        c = c0
        while c < c1:
            n = min(step, c1 - c)
            out.append((c, n))
            c += n
        return out

    if ("A" in phases) or ("S" in phases):
        mwc = A.mark()
        wcs = A.alloc([16, 128], F32)
        for part in range(2):
            P.ld("sync", wcs, (wc_re_d if part == 0 else wc_im_d)[:, :, :], reads=[], writes=["wcs"])
            if part == 0:
                P.cp("vector", WC[:, 0, :, :], wcs, reads=["wcs"], writes=["WC"])
            else:
                P.ts("vector", WC[:, 1, :, :], wcs, -1.0, None, ALU.mult, reads=["wcs"], writes=["WC"])
        P.barrier()
        A.release(mwc)
        m0 = A.mark()
        sm = A.alloc([16, 3], F32)
        bre = A.alloc([16, 16], F32)
        bim = A.alloc([16, 16], F32)
        bbr = A.alloc([16, 16], F32)
        bbi = A.alloc([16, 16], F32)
        t3a = A.alloc([16, 16], F32)
        t3b = A.alloc([16, 16], F32)
        lps = A.alloc([16, 8], F32)
        lpc = A.alloc([16, 8], F32)
        lpm = A.alloc([16, 8], F32)
        lpr = A.alloc([16, 8], F32)
        lpi = A.alloc([16, 8], F32)
        e8t = A.alloc([4, 16, 4], F32)
        ZP = A.alloc([16, 128], F32)
        W8 = A.alloc([2, 8, 16 * 128], BF16) if "A" in phases else None
        P.ld("sync", sm, ssm_small[:, :, :], writes=["sm"])
        P.ld("sync", bre, b_re_d[:, :, :], writes=["bre"])
        P.ld("sync", bim, b_im_d[:, :, :], writes=["bim"])
        dtc = tmpc[:, 0, :]
        P.act(dtc, sm[:, :, 2], AF.Exp, reads=["sm"], writes=["dtc"])
        P.tt("vector", are, sm[:, :, 0], dtc, ALU.mult, reads=["sm", "dtc"], writes=["are"])
        P.tt("vector", th, sm[:, :, 1], dtc, ALU.mult, reads=["sm", "dtc"], writes=["th"])
        P.act(r1, are, AF.Exp, reads=["are"], writes=["r1"])
        for k in range(8):
            P.act(lpm[:, :, k], are, AF.Exp, reads=["are"], writes=["lpm"], scale=float(k))
        TK = ["tmpc"]
        ph, ph2, pp, ta, tb, tcc = (tmpc[:, i, :] for i in range(1, 7))
        P.ts("vector", ph, th, 1.0 / 16.0, None, ALU.mult, reads=["th"] + TK, writes=TK)
        P.tt("vector", ph2, ph, ph, ALU.mult, reads=TK, writes=TK)
        cur_re, cur_im = PWr[:, 0, :], PWi[:, 0, :]
        sc_ = [1.0, -1.0 / 6, 1.0 / 120, -1.0 / 5040, 1.0 / 362880, -1.0 / 39916800, 1.0 / 6227020800.0]
        cc_ = [1.0, -0.5, 1.0 / 24, -1.0 / 720, 1.0 / 40320, -1.0 / 3628800, 1.0 / 479001600, -1.0 / 87178291200.0]
        for coef, dst, mulphi in ((sc_, cur_im, True), (cc_, cur_re, False)):
            P.memset("vector", pp, coef[-1], writes=TK)
            for cf in coef[-2::-1]:
                P.tt("vector", pp, pp, ph2, ALU.mult, reads=TK, writes=TK)
                P.ts("vector", pp, pp, cf, None, ALU.add, reads=TK, writes=TK)
            if mulphi:
                P.tt("vector", dst, pp, ph, ALU.mult, reads=TK, writes=["PW"])
            else:
                P.cp("vector", dst, pp, reads=TK, writes=["PW"])

        def csq_norm(o_re, o_im, a_re, a_im):
            P.tt("vector", ta, a_re, a_re, ALU.mult, reads=TK + ["PW"], writes=TK)
            P.tt("vector", tb, a_im, a_im, ALU.mult, reads=TK + ["PW"], writes=TK)
            P.tt("vector", tcc, a_re, a_im, ALU.mult, reads=TK + ["PW"], writes=TK)
            P.tt("vector", o_re, ta, tb, ALU.subtract, reads=TK, writes=["PW"])
            P.ts("vector", o_im, tcc, 2.0, None, ALU.mult, reads=TK, writes=["PW"])
            P.tt("vector", ta, o_re, o_re, ALU.mult, reads=TK + ["PW"], writes=TK)
            P.tt("vector", tb, o_im, o_im, ALU.mult, reads=TK + ["PW"], writes=TK)
            P.tt("vector", ta, ta, tb, ALU.add, reads=TK, writes=TK)
            P.act(ta, ta, AF.Sqrt, reads=TK, writes=TK)
            P.recip(ta, ta, reads=TK, writes=TK)
            P.tt("vector", o_re, o_re, ta, ALU.mult, reads=TK + ["PW"], writes=["PW"])
            P.tt("vector", o_im, o_im, ta, ALU.mult, reads=TK + ["PW"], writes=["PW"])

        for _ in range(4):
            csq_norm(cur_re, cur_im, cur_re, cur_im)
        for k in range(1, NPW):
            csq_norm(PWr[:, k, :], PWi[:, k, :], PWr[:, k - 1, :], PWi[:, k - 1, :])

        def phasor_table(Er, Ei, T, k0, tmps, ekey):
            P.memset("vector", Er[:, :, 0:1], 1.0, writes=[ekey])
            P.memset("vector", Ei[:, :, 0:1], 0.0, writes=[ekey])
            n = 1
            k = k0
            while n < T:
                pr_b = PWr[:, k, :].rearrange("p (a o) -> p a o", o=1).to_broadcast([128, 16, n])
                pi_b = PWi[:, k, :].rearrange("p (a o) -> p a o", o=1).to_broadcast([128, 16, n])
                q = [t[:, :, 0:n] for t in tmps]
                P.tt("vector", q[0], Er[:, :, 0:n], pr_b, ALU.mult, reads=[ekey, "PW", "etmp"], writes=["etmp"])
                P.tt("vector", q[1], Ei[:, :, 0:n], pi_b, ALU.mult, reads=[ekey, "PW", "etmp"], writes=["etmp"])
                P.tt("vector", q[2], Er[:, :, 0:n], pi_b, ALU.mult, reads=[ekey, "PW", "etmp"], writes=["etmp"])
                P.tt("vector", q[3], Ei[:, :, 0:n], pr_b, ALU.mult, reads=[ekey, "PW", "etmp"], writes=["etmp"])
                P.tt("vector", Er[:, :, n:2 * n], q[0], q[1], ALU.subtract, reads=["etmp"], writes=[ekey])
                P.tt("vector", Ei[:, :, n:2 * n], q[2], q[3], ALU.add, reads=["etmp"], writes=[ekey])
                n *= 2
                k += 1

        phasor_table(lpc, lps, 8, 0, [e8t[:, i, :, :] for i in range(4)], "lpcs")
        fl = lambda t: t.rearrange("p a b -> p (a b)")
        P.tt("vector", fl(lpr), fl(lpm), fl(lpc), ALU.mult, reads=["lpm", "lpcs"], writes=["lpr"])
        P.tt("vector", fl(lpi), fl(lpm), fl(lps), ALU.mult, reads=["lpm", "lpcs"], writes=["lpi"])
        P.cp("vector", lb_re, lpr[:, :, 1], reads=["lpr"], writes=["lb_re"])
        P.cp("vector", lb_im, lpi[:, :, 1], reads=["lpi"], writes=["lb_im"])
        nre, d2, gre, gim, ta, tb = (tmpc[:, i, :] for i in range(1, 7))
        KT_ = ["tmpc"]
        P.ts("vector", nre, lb_re, -1.0, None, ALU.add, reads=["lb_re"] + KT_, writes=KT_)
        P.tt("vector", d2, sm[:, :, 0], sm[:, :, 0], ALU.mult, reads=["sm"] + KT_, writes=KT_)
        P.tt("vector", ta, sm[:, :, 1], sm[:, :, 1], ALU.mult, reads=["sm"] + KT_, writes=KT_)
        P.tt("vector", d2, d2, ta, ALU.add, reads=KT_, writes=KT_)
        P.recip(d2, d2, reads=KT_, writes=KT_)
        P.tt("vector", gre, nre, sm[:, :, 0], ALU.mult, reads=["sm"] + KT_, writes=KT_)
        P.tt("vector", ta, lb_im, sm[:, :, 1], ALU.mult, reads=["sm", "lb_im"] + KT_, writes=KT_)
        P.tt("vector", gre, gre, ta, ALU.add, reads=KT_, writes=KT_)
        P.tt("vector", gre, gre, d2, ALU.mult, reads=KT_, writes=KT_)
        P.tt("vector", gim, lb_im, sm[:, :, 0], ALU.mult, reads=["sm", "lb_im"] + KT_, writes=KT_)
        P.tt("vector", ta, nre, sm[:, :, 1], ALU.mult, reads=["sm"] + KT_, writes=KT_)
        P.tt("vector", gim, gim, ta, ALU.subtract, reads=KT_, writes=KT_)
        P.tt("vector", gim, gim, d2, ALU.mult, reads=KT_, writes=KT_)
        bc = lambda t: t.rearrange("p (a o) -> p a o", o=1).to_broadcast([128, 16, 16])
        P.tt("vector", t3a, bre, bc(gre), ALU.mult, reads=["bre"] + KT_, writes=["t3a"])
        P.tt("vector", t3b, bim, bc(gim), ALU.mult, reads=["bim"] + KT_, writes=["t3b"])
        P.tt("vector", bbr, t3a, t3b, ALU.subtract, reads=["t3a", "t3b"], writes=["bbr"])
        P.tt("vector", t3a, bim, bc(gre), ALU.mult, reads=["bim", "bbr"] + KT_, writes=["t3a"])
        P.tt("vector", t3b, bre, bc(gim), ALU.mult, reads=["bre", "bbr"] + KT_, writes=["t3b"])
        P.tt("vector", bbi, t3a, t3b, ALU.add, reads=["t3a", "t3b"], writes=["bbi"])
        P.memset("gpsimd", ZP, 0.0, writes=["ZP"])
        svals = (range(8) if "A" in phases else [7]) if not (DBG_SKIP & 16) else [7]
        for s in svals:
            k = 7 - s
            fr = lambda t: t[:, :, k:k + 1].to_broadcast([128, 16, 16])
            for part in range(2):
                if part == 0:
                    P.tt("vector", t3a, bbr, fr(lpr), ALU.mult, reads=["bbr", "lpr", "ZP"], writes=["t3a"])
                    P.tt("vector", t3b, bbi, fr(lpi), ALU.mult, reads=["bbi", "lpi", "ZP"], writes=["t3b"])
                    P.tt("vector", t3a, t3a, t3b, ALU.subtract, reads=["t3a", "t3b"], writes=["t3a"])
                else:
                    P.tt("vector", t3a, bbi, fr(lpr), ALU.mult, reads=["bbi", "lpr", "ZP"], writes=["t3a"])
                    P.tt("vector", t3b, bbr, fr(lpi), ALU.mult, reads=["bbr", "lpi", "ZP"], writes=["t3b"])
                    P.tt("vector", t3a, t3a, t3b, ALU.add, reads=["t3a", "t3b"], writes=["t3a"])
                for pm in range(4):
                    for gl in range(2):
                        P.cp("vector", ZP[64 * gl:64 * gl + 64, pm::4, 32 * pm + 16 * gl:32 * pm + 16 * gl + 16],
                             t3a[64 * gl:64 * gl + 64, pm::4, :], reads=["t3a"], writes=["ZP"])
                for q4 in range(4):
                    bank, bkey = nextbank()
                    for j in range(4):
                        pr = q4 * 4 + j
                        P.tr(bank[:, j * 128:(j + 1) * 128], ZP[:, pr, :], ident32, reads=["ZP", "ident32"], writes=[bkey])
                    if "A" in phases:
                        P.cp("vector", W8[:, part, s, q4 * 512:(q4 + 1) * 512], bank, reads=[bkey], writes=["W8"])
                    if s == 7:
                        P.cp("vector", WB[:, part, q4 * 4:(q4 + 1) * 4, :], bank.rearrange("p (a b) -> p a b", a=4), reads=[bkey], writes=["WB"])
        P.memset("gpsimd", gi_re, 0.0, writes=["gi"])
        P.memset("gpsimd", gi_im, 0.0, writes=["gi"])

    if "A" in phases and not (DBG_SKIP & 8):
        J = 64
        r8 = A.alloc([16], F32)
        C8 = A.alloc([16, J], F32)
        S8 = A.alloc([16, J], F32)
        e8tmp = [A.alloc([16, J // 2], F32) for _ in range(4)]
        P.act(r8, are, AF.Exp, reads=["are"], writes=["r8"], scale=8.0)
        if not (DBG_SKIP & 1):
            phasor_table(C8, S8, J, 3, e8tmp, "CS8")
        cJ = PWr[:, 9, :]
        sJ = PWi[:, 9, :]
        frontend_alloc()
        wust = A.alloc([8, 128], F32)
        wu16 = A.alloc([4, 8, 128], BF16)
        for ct in range(0 if (DBG_SKIP & 2) else 4):
            P.ld("sync", wust, w_tiles[TI[("u", ct)], :, :, :], writes=["wust"])
            P.tt("gpsimd", wu16[:, ct, :, :], wust, gm.rearrange("p (a o) -> p a o", o=1).to_broadcast([128, 8, 128]), ALU.mult, reads=["wust", "gm"], writes=["wu16"])
        CH = 512
        hTc = [A.alloc([8, CH], BF16) for _ in range(2)]
        uTc = [A.alloc([4, CH], BF16)] * 2
        Rre = A.alloc([J], F32)
        Rim = A.alloc([J], F32)
        Gre = A.alloc([J], F32)
        Gim = A.alloc([J], F32)
        q1 = A.alloc([J], F32)
        q2 = A.alloc([J], F32)
        nch = min(TPRI // CH, DBG_NCH)
        for ch in range(nch):
            par = ch % 2
            hk = ("hTc", par)
            uk = "uTc"
            for t in range(min(CH // 128, DBG_NT)):
                r0 = ch * CH + t * 128
                frontend(x_all[r0:r0 + 128, :], 128, hTc[par], hk, t * 128)
            for ct in range(4 if DBG_STAGE >= 2 else 0):
                for (c0, n) in chunks(0, CH):
                    bank, bkey = nextbank(banks)
                    proj(wu16[:, ct, :, :], "wu16", hTc[par], hk, c0, n, bank, bkey)
                    P.cp("scalar", uTc[par][:, ct, c0:c0 + n], bank[:, 0:n], reads=[bkey], writes=[uk])
            for pr in range(16 if DBG_STAGE >= 3 else 0):
                ct = pr // 4
                bank, bkey = nextbank(banks)
                xre = bank[:, 0:J]
                xim = bank[:, J:2 * J]
                for part, dst in ((0, xre), (1, xim)):
                    for s in range(8):
                        P.mm(dst, W8[:, part, s, pr * 128:(pr + 1) * 128], uTc[par][:, ct, s:CH:8], start=(s == 0), stop=(s == 7),
                             reads=["W8", uk], writes=[bkey])
                if DBG_STAGE < 4:
                    continue
                c8 = C8[:, pr, :]
                s8 = S8[:, pr, :]
                P.tt("vector", q1, xre, c8, ALU.mult, reads=[bkey, "CS8"], writes=["q1"])
                P.tt("vector", q2, xim, s8, ALU.mult, reads=[bkey, "CS8"], writes=["q2"])
                P.tt("gpsimd", Rre, q1, q2, ALU.add, reads=["q1", "q2"], writes=["Rre"])
                P.tt("vector", q1, xim, c8, ALU.mult, reads=[bkey, "CS8", "Rre"], writes=["q1"])
                P.tt("vector", q2, xre, s8, ALU.mult, reads=[bkey, "CS8", "Rre"], writes=["q2"])
                P.tt("gpsimd", Rim, q1, q2, ALU.subtract, reads=["q1", "q2"], writes=["Rim"])
                P.scan(Gre, r8[:, pr:pr + 1].to_broadcast([128, J]), Rre, gi_re[:, pr:pr + 1], reads=["Rre", "r8", "gi"], writes=["Gre"])
                P.scan(Gim, r8[:, pr:pr + 1].to_broadcast([128, J]), Rim, gi_im[:, pr:pr + 1], reads=["Rim", "r8", "gi"], writes=["Gim"])
                ta_ = tmpc[:, 0, pr:pr + 1]
                tb_ = tmpc[:, 1, pr:pr + 1]
                P.ts("vector", ta_, Gim[:, J - 1:J], sJ[:, pr:pr + 1], None, ALU.mult, reads=["Gim", "PW"], writes=["tmpc"])
                P.ts("vector", tb_, Gre[:, J - 1:J], sJ[:, pr:pr + 1], None, ALU.mult, reads=["Gre", "PW"], writes=["tmpc"])
                P.stt("vector", gi_re[:, pr:pr + 1], Gre[:, J - 1:J], cJ[:, pr:pr + 1], ta_, ALU.mult, ALU.subtract, reads=["Gre", "PW", "tmpc"], writes=["gi"])
                P.stt("vector", gi_im[:, pr:pr + 1], Gim[:, J - 1:J], cJ[:, pr:pr + 1], tb_, ALU.mult, ALU.add, reads=["Gim", "PW", "tmpc"], writes=["gi"])
        if DBG_SKIP & 4:
            phases = phases.replace("A", "a")
        c7 = lpc[:, :, 7]
        s7 = lps[:, :, 7]
        ta, tb = tmpc[:, 2, :], tmpc[:, 3, :]
        P.tt("vector", ta, gi_im, s7, ALU.mult, reads=["gi", "lpcs"], writes=["tmpc"])
        P.tt("vector", tb, gi_re, s7, ALU.mult, reads=["gi", "lpcs"], writes=["tmpc"])
        P.tt("vector", gi_re, gi_re, c7, ALU.mult, reads=["gi", "lpcs"], writes=["gi"])
        P.tt("vector", gi_re, gi_re, ta, ALU.add, reads=["gi", "tmpc"], writes=["gi"])
        P.tt("vector", gi_im, gi_im, c7, ALU.mult, reads=["gi", "lpcs"], writes=["gi"])
        P.tt("vector", gi_im, gi_im, tb, ALU.subtract, reads=["gi", "tmpc"], writes=["gi"])
    P.barrier()
    A.release(base_mark)
    if dbg == "A":
        dd = A.alloc([4096], F32)
        P.memset("gpsimd", dd, 0.0, writes=["dd"])
        P.cp("vector", dd[:, 0:16], gi_re, reads=["gi"], writes=["dd"])
        P.cp("vector", dd[:, 16:32], gi_im, reads=["gi"], writes=["dd"])
        P.cp("vector", dd[:, 32:48], th, reads=["th"], writes=["dd"])
        P.cp("vector", dd[:, 48:64], r1, reads=["r1"], writes=["dd"])
        P.cp("vector", dd[:, 64:80], lb_re, reads=["lb_re"], writes=["dd"])
        P.cp("vector", dd[:, 80:96], lb_im, reads=["lb_im"], writes=["dd"])
        P.cp("vector", dd[:, 128:128 + 2048], WB[:, 0, :, :].rearrange("p a b -> p (a b)"), reads=["WB"], writes=["dd"])
        P.ld("sync", dbg_o[:, :], dd, reads=["dd"])


    attnT = A.alloc([4, TOWN + NDEC], BF16)
    pers_mark = A.mark()
    hTo = A.alloc([8, TOWN + NDEC], BF16)
    own_mark = A.mark()
    hTh = A.alloc([8, TOWN], BF16)
    hT_mark = A.mark()

    class _HT:
        def __getitem__(self, idx):
            p, kt, cs = idx
            st = cs.start
            sd = cs.step or 1
            cnt = (cs.stop - st + sd - 1) // sd
            stop = st + (cnt - 1) * sd + 1
            if st < TOWN:
                assert stop <= TOWN
                return hTh[p, kt, slice(st, stop, sd)]
            return hTo[p, kt, slice(st - TOWN, stop - TOWN, sd)]

    hT = _HT()

    if "B" in phases:
        frontend_alloc()
        for t in range(32):
            r0 = TPRI - TOWN + t * 128
            if t < 16:
                frontend(x_all[r0:r0 + 128, :], 128, hTh, "hT", t * 128)
            else:
                frontend(x_all[r0:r0 + 128, :], 128, hTo, "hT", (t - 16) * 128)
        frontend(x_dec[:, :], NDEC, hTo, "hT", TOWN)
        P.barrier()
        A.release(hT_mark)
        ropeC = A.alloc([TALL], BF16)
        ropeS = A.alloc([TALL], BF16)
        m_r = A.mark()
        rtmp = A.alloc([1028], F32)
        for (tab, src, key) in ((ropeC, rope_c_d, "ropeC"), (ropeS, rope_s_d, "ropeS")):
            for (c0, n) in chunks(0, TALL, 1028):
                P.ld("sync", rtmp[:, 0:n], src[:, c0:c0 + n], writes=["rtmp"])
                P.cp("vector", tab[:, c0:c0 + n], rtmp[:, 0:n], reads=["rtmp"], writes=[key])
        P.barrier()
        A.release(m_r)
        mask2 = A.alloc([256], BF16)
        maskh = A.alloc([128], BF16)
        mtmp = A.alloc([256], F32)
        P.ld("sync", mtmp, mask2_d[:, :], writes=["mtmp"])
        P.cp("vector", mask2, mtmp, reads=["mtmp"], writes=["mask2"])
        P.ld("sync", mtmp[:, 0:128], maskh_d[:, :], reads=["mask2"], writes=["mtmp"])
        P.cp("vector", maskh, mtmp[:, 0:128], reads=["mtmp"], writes=["maskh"])
        QTd = A.alloc([12, NDEC], BF16)
        KTd = A.alloc([12, NDEC], BF16)
        VTd = A.alloc([12, NDEC], F32)
        m_dec = A.mark()
        wtile_alloc()
        QT = [A.alloc([TOWN + NDEC], BF16) for g in range(3)]
        KT = [A.alloc([TALL - KSTART[g]], BF16) for g in range(3)]
        NBLK = (17, 20, 32)
        VG = [A.alloc([NBLK[g], 192], BF16) for g in range(3)]
        for g in range(3):
            P.memset("gpsimd", VG[g][:, :, 64:128], 1.0, writes=[("VG", g)])
        t1b = [A.alloc([512], F32) for _ in range(2)]
        t2b = [A.alloc([512], F32) for _ in range(2)]
        PTb = [A.alloc([256], BF16) for _ in range(3)]
        Rb = [A.alloc([512], F32) for _ in range(2)]
        ev = {"i": 0}

        def rope_evac(bA, kA, bB, kB, c0, n, dst, dkey):
            i = ev["i"] % 2
            ev["i"] += 1
            P.tt("vector", t1b[i][:, 0:n], bA[:, 0:n], ropeC[:, c0:c0 + n], ALU.mult, reads=[kA, "ropeC"], writes=[("t1", i)])
            P.tt("vector", t2b[i][:, 0:n], bB[:, 0:n], ropeS[:, c0:c0 + n], ALU.mult, reads=[kB, "ropeS"], writes=[("t2", i)])
            P.tt("gpsimd", dst, t1b[i][:, 0:n], t2b[i][:, 0:n], ALU.add, reads=[("t1", i), ("t2", i)], writes=[dkey])

        def vblocks(g):
            out = []
            if g == 0:
                for b in range(17):
                    out.append((1920 + 128 * b, 1))
            elif g == 1:
                for r in range(4):
                    for kbi in range(5):
                        out.append((1536 + 512 * kbi + r, 4))
            else:
                for r in range(16):
                    for kbi in range(2):
                        out.append((2048 * kbi + r, 16))
            return out

        pt_cnt = {"i": 0}
        for hp in range(4 if DBG_STAGE >= 9 else 1):
            for g in range(3):
                wm, kwm = load_wtile(TI[("q", g, hp)])
                wp, kwp = load_wtile(TI[("qp", g, hp)])
                for (c0, n) in chunks(TOWN, TALL):
                    bA, kA = nextbank()
                    proj(wm, kwm, hT, "hT", c0, n, bA, kA)
                    bB, kB = nextbank()
                    proj(wp, kwp, hT, "hT", c0, n, bB, kB)
                    rope_evac(bA, kA, bB, kB, c0, n, QT[g][:, c0 - TOWN:c0 - TOWN + n], ("QT", g))
                P.cp("gpsimd", QTd[:, g * 4 + hp, :], QT[g][:, TOWN:TOWN + NDEC], reads=[("QT", g)], writes=["QTd"])
                wm, kwm = load_wtile(TI[("k", g, hp)])
                wp, kwp = load_wtile(TI[("kp", g, hp)])
                ks = KSTART[g]
                for (c0, n) in chunks(ks, TOWN) + chunks(TOWN, TALL):
                    bA, kA = nextbank()
                    proj(wm, kwm, hT, "hT", c0, n, bA, kA)
                    bB, kB = nextbank()
                    proj(wp, kwp, hT, "hT", c0, n, bB, kB)
                    rope_evac(bA, kA, bB, kB, c0, n, KT[g][:, c0 - ks:c0 - ks + n], ("KT", g))
                P.cp("gpsimd", KTd[:, g * 4 + hp, :], KT[g][:, TEXT - ks:TEXT - ks + NDEC], reads=[("KT", g)], writes=["KTd"])
                wv, kwv = load_wtile(TI[("v", g, hp)])
                blks = vblocks(g)
                for b0 in range(0, len(blks), 4):
                    nb = min(4, len(blks) - b0)
                    bank, bkey = nextbank()
                    for j in range(nb):
                        st, sd = blks[b0 + j]
                        for kt in range(8):
                            P.mm(bank[:, j * 128:(j + 1) * 128], hT[:, kt, st:st + 128 * sd:sd], wv[:, kt, :], start=(kt == 0), stop=(kt == 7),
                                 reads=["hT", kwv], writes=[bkey])
                    bv = bank.rearrange("p (a b) -> p a b", a=4)
                    P.cp("scalar", VG[g][:, b0:b0 + nb, 0:64], bv[:, 0:nb, 0:64], reads=[bkey], writes=[("VG", g)])
                    P.cp("scalar", VG[g][:, b0:b0 + nb, 128:192], bv[:, 0:nb, 64:128], reads=[bkey], writes=[("VG", g)])
                bank, bkey = nextbank()
                proj(wv, kwv, hT, "hT", TEXT, NDEC, bank, bkey)
                P.cp("scalar", VTd[:, g * 4 + hp, :], bank[:, 0:NDEC], reads=[bkey], writes=["VTd"])
            for hl in range(2):
                hr = slice(64 * hl, 64 * hl + 64)
                nrows = slice(64 * hl, 64 * hl + 64)
                lrows = slice(64 - 64 * hl, 128 - 64 * hl)
                vsel = slice(0, 128) if hl == 0 else slice(64, 192)
                for k4 in range(4):
                    P.mm(pacc[:, 512 * k4:512 * (k4 + 1)], zeros16, hTo[:, 0, 0:512], start=True, stop=False,
                         reads=["zeros16", "hT"], writes=["pacc%d" % k4])
                for g in range(3):
                    ks = KSTART[g]
                    if g == 0:
                        units = [(0, kb) for kb in range(-1, 16)]
                        nq = 16
                    elif g == 1:
                        units = [(r, kb) for r in range(4) for kb in range(-1, 4)]
                        nq = 4
                    else:
                        units = [(r, kb) for r in range(16) for kb in range(-1, 1)]
                        nq = 1
                    for (r, kb) in units:
                        if g == 0:
                            blk = kb + 1
                            kst, ksd = 2048 + 128 * kb, 1
                        elif g == 1:
                            blk = r * 5 + kb + 1
                            kst, ksd = 2048 + 512 * kb + r, 4
                        else:
                            blk = r * 2 + kb + 1
                            kst, ksd = 2048 + 2048 * kb + r, 16
                        kcols = KT[g][hr, kst - ks:kst - ks + 128 * ksd:ksd]
                        roles = []
                        if kb >= 0:
                            roles.append((kb, mask2[:, 0:128], "mask2"))
                        if kb + 1 < nq:
                            roles.append((kb + 1, (maskh if kb == -1 else mask2[:, 128:256]), ("maskh" if kb == -1 else "mask2")))
                        sb, skey = nextbank()
                        half = (pt_cnt["i"] // 3) % 2
                        pti = pt_cnt["i"] % 3
                        pt_cnt["i"] += 1
                        S = sb[:, 0:256]
                        for ri, (qb, mk, mkey) in enumerate(roles):
                            if g == 0:
                                qst, qsd = 128 * qb, 1
                            elif g == 1:
                                qst, qsd = 512 * qb + r, 4
                            else:
                                qst, qsd = r, 16
                            qcols = QT[g][hr, qst:qst + 128 * qsd:qsd]
                            P.mm(S[:, ri * 128:(ri + 1) * 128], kcols, qcols, start=True, stop=False, reads=[("KT", g), ("QT", g)], writes=[skey])
                            P.mm(S[:, ri * 128:(ri + 1) * 128], ident16, mk, start=False, stop=True, reads=["ident16", mkey], writes=[skey])
                        nr = len(roles)
                        PT = PTb[pti]
                        P.act(PT[:, 0:nr * 128], S[:, 0:nr * 128], AF.Exp, reads=[skey], writes=[("PT", pti)], scale=0.125)
                        for ri, (qb, mk, mkey) in enumerate(roles):
                            lhs = VG[g][:, blk, vsel]
                            if g == 0:
                                P.mm(pacc[:, 128 * qb:128 * qb + 128], lhs, PT[:, ri * 128:(ri + 1) * 128], start=False, stop=False,
                                     reads=[("VG", g), ("PT", pti)], writes=["pacc%d" % (qb // 4)])
                            elif g == 1:
                                P.mm(pacc[:, 512 * qb + r:512 * qb + 512:4], lhs, PT[:, ri * 128:(ri + 1) * 128], start=False, stop=False,
                                     reads=[("VG", g), ("PT", pti)], writes=["pacc%d" % qb])
                            else:
                                for k4 in range(4):
                                    P.mm(pacc[:, 512 * k4 + r:512 * k4 + 512:16], lhs, PT[:, ri * 128 + 32 * k4:ri * 128 + 32 * k4 + 32], start=False, stop=False,
                                         reads=[("VG", g), ("PT", pti)], writes=["pacc%d" % k4])
                for k4 in range(4):
                    Rk = Rb[k4 % 2]
                    P.recip(Rk[lrows, :], pacc[lrows, 512 * k4:512 * (k4 + 1)], reads=["pacc%d" % k4], writes=[("Rb", k4 % 2)])
                    P.tt("vector", attnT[nrows, hp, 512 * k4:512 * (k4 + 1)], pacc[nrows, 512 * k4:512 * (k4 + 1)], Rk[lrows, :], ALU.mult,
                         reads=["pacc%d" % k4, ("Rb", k4 % 2)], writes=["attnT"])

        P.barrier()
        A.release(m_dec)
        if "F" in phases:
            m_f = A.mark()
            kcb = [A.alloc([2, 512], F32) for _ in range(2)]
            prod = A.alloc([512], F32)
            s8 = A.alloc([8], F32)
            p8 = A.alloc([8], F32)
            om = A.alloc([512], F32)
            lcs = A.alloc([1], F32)
            EM = A.alloc([512], F32)
            bones = A.alloc([128], F32)
            pq = A.alloc([NDEC], F32)
            psf = A.alloc([NDEC], F32)
            NDs = A.alloc([4, NDEC], F32)
            LDs = A.alloc([4, NDEC], F32)
            tq = A.alloc([NDEC], F32)
            P.ld("sync", EM[0:8, :], em_d[:, :], writes=["EM"])
            P.ld("sync", bones, bones_d[:, :], writes=["bones"])
            P.memset("gpsimd", NDs, 0.0, writes=["NDs"])
            P.memset("gpsimd", LDs, 0.0, writes=["LDs"])
            ndld = pacc[:, 1536:1664]
            NDp = ndld[:, 0:64].rearrange("p (a b) -> p a b", a=4)
            LDp = ndld[:, 64:128].rearrange("p (a b) -> p a b", a=4)
            P.mm(ndld, zeros16, hTo[:, 0, 0:128], start=True, stop=False, reads=["zeros16", "hT"], writes=["pacc3"])
            ci = 0
            for b in range(NDEC):
                for g in range(3):
                    par = ci % 2
                    ci += 1
                    kc = kcb[par]
                    P.ld("sync", kc, caches[g][b, 0:LC[g]:DIL[g], :, :], writes=[("kc", par)])
                    qb_, qkey = nextbank()
                    for hp in range(4):
                        P.mm(qb_[:, hp * 128:(hp + 1) * 128], QTd[:, g * 4 + hp, b:b + 1].to_broadcast([128, 128]), ident16, reads=["QTd", "ident16"], writes=[qkey])
                    P.tt("vector", prod, qb_, kc[:, 0, :], ALU.mult, reads=[qkey, ("kc", par)], writes=["prod"])
                    P.red("vector", s8, prod.rearrange("p (h e) -> p h e", e=64), ALU.add, AX.X, reads=["prod"], writes=["s8"])
                    P.act(p8, s8, AF.Exp, reads=["s8"], writes=["p8"], scale=0.125)
                    ob, okey = nextbank()
                    P.mm(ob[0:8, 0:512], p8[:, 0:8], kc[:, 1, :], reads=["p8", ("kc", par)], writes=[okey])
                    lb_, lkey = nextbank()
                    P.mm(lb_[0:8, 0:1], p8[:, 0:8], ones32[:, 0:1], reads=["p8", "ones32"], writes=[lkey])
                    P.tt("vector", om[0:8, :], ob[0:8, 0:512], EM[0:8, :], ALU.mult, reads=[okey, "EM"], writes=["om"])
                    P.cp("vector", lcs[0:8, :], lb_[0:8, 0:1], reads=[lkey], writes=["lcs"])
                    for hp in range(4):
                        P.mm(NDp[:, hp, b:b + 1], om[0:8, hp * 128:(hp + 1) * 128], ones32[0:8, 0:1], start=False, stop=False, reads=["om", "ones32"], writes=["pacc3"])
                        P.mm(LDp[:, hp, b:b + 1], EM[0:8, hp * 128:(hp + 1) * 128], lcs[0:8, 0:1], start=False, stop=False, reads=["EM", "lcs"], writes=["pacc3"])
            for g in range(3):
                for hp in range(4):
                    idx = g * 4 + hp
                    P.tt("vector", pq, QTd[:, idx, :], KTd[:, idx, :], ALU.mult, reads=["QTd", "KTd"], writes=["pq"])
                    sb_, skey = nextbank()
                    P.mm(sb_[:, 0:NDEC], bones, pq, reads=["bones", "pq"], writes=[skey])
                    P.act(psf, sb_[:, 0:NDEC], AF.Exp, reads=[skey], writes=["psf"], scale=0.125)
                    P.tt("vector", tq, psf, VTd[:, idx, :], ALU.mult, reads=["psf", "VTd"], writes=["tq"])
                    P.tt("vector", NDs[:, hp, :], NDs[:, hp, :], tq, ALU.add, reads=["NDs", "tq"], writes=["NDs"])
                    P.tt("vector", LDs[:, hp, :], LDs[:, hp, :], psf, ALU.add, reads=["LDs", "psf"], writes=["LDs"])
            P.tt("vector", NDs, NDs, NDp, ALU.add, reads=["NDs", "pacc3"], writes=["NDs"])
            P.tt("vector", LDs, LDs, LDp, ALU.add, reads=["LDs", "pacc3"], writes=["LDs"])
            P.recip(LDs, LDs, reads=["LDs"], writes=["LDs"])
            P.tt("vector", attnT[:, :, TOWN:TOWN + NDEC], NDs, LDs, ALU.mult, reads=["NDs", "LDs"], writes=["attnT"])
            P.barrier()
            A.release(m_f)
        if "E" in phases:
            m_e = A.mark()
            rtm = A.alloc([17, 16], F32)
            wkst = A.alloc([8, 512], F32)
            wk16 = A.alloc([8, 512], BF16)
            o32 = [A.alloc([8, 64], F32) for _ in range(2)]
            rt = [A.alloc([8, 8], F32) for _ in range(4)]
            P.ld("sync", rtm, rope_tm_d[:, :, :], writes=["rtm"])
            oi = 0
            for kv in range(2):
                for g in range(3):
                    P.ld("sync", wkst, w_kv[kv * 3 + g, :, :, :], writes=["wkst"])
                    P.tt("gpsimd", wk16, wkst, gm.rearrange("p (a o) -> p a o", o=1).to_broadcast([128, 8, 512]), ALU.mult, reads=["wkst", "gm"], writes=["wk16"])
                    tl = {0: [15], 1: [12, 13, 14, 15], 2: list(range(16))}[g] + [16]
                    for t in tl:
                        ntok = 128 if t < 16 else NDEC
                        par = oi % 2
                        oi += 1
                        ov = o32[par]
                        okey2 = ("o32", par)
                        bank, bkey = nextbank()
                        for kt in range(8):
                            P.mm(bank[0:ntok, 0:512], hTo[:, kt, t * 128:t * 128 + ntok], wk16[:, kt, :], start=(kt == 0), stop=(kt == 7), reads=["hT", "wk16"], writes=[bkey])
                        P.cp("scalar", ov[0:ntok, :, :], bank[0:ntok, 0:512].rearrange("p (h e) -> p h e", e=64), reads=[bkey], writes=[okey2])
                        if kv == 0:
                            cb = rtm[0:ntok, t, 0:8].rearrange("p (o e) -> p o e", o=1).to_broadcast([ntok, 8, 8])
                            sbb = rtm[0:ntok, t, 8:16].rearrange("p (o e) -> p o e", o=1).to_broadcast([ntok, 8, 8])
                            x1v, x2v = ov[0:ntok, :, 0:8], ov[0:ntok, :, 8:16]
                            P.tt("vector", rt[0][0:ntok], x1v, cb, ALU.mult, reads=[okey2, "rtm"], writes=["rt0"])
                            P.tt("vector", rt[1][0:ntok], x2v, sbb, ALU.mult, reads=[okey2, "rtm"], writes=["rt1"])
                            P.tt("vector", rt[2][0:ntok], x2v, cb, ALU.mult, reads=[okey2, "rtm"], writes=["rt2"])
                            P.tt("vector", rt[3][0:ntok], x1v, sbb, ALU.mult, reads=[okey2, "rtm"], writes=["rt3"])
                            P.tt("vector", x1v, rt[0][0:ntok], rt[1][0:ntok], ALU.subtract, reads=["rt0", "rt1"], writes=[okey2])
                            P.tt("vector", x2v, rt[2][0:ntok], rt[3][0:ntok], ALU.add, reads=["rt2", "rt3"], writes=[okey2])
                        src = ov[0:ntok, :, :].rearrange("p h e -> p (h e)")
                        if t < 16:
                            r0 = (t - tl[0]) * 128
                            P.ld("sync", kvp[g][r0:r0 + 128, kv, :], src, reads=[okey2])
                        else:
                            P.ld("sync", kvs[g][:, LC[g] - 1, kv, :], src, reads=[okey2])
            P.barrier()
            A.release(m_e)
        if dbg == "B":
            P.barrier()
            A.release(m_dec)
            dd = A.alloc([4096], F32)
            P.memset("gpsimd", dd, 0.0, writes=["dd"])
            P.cp("vector", dd[:, 0:2048], attnT[:, 0, 0:2048], reads=["attnT"], writes=["dd"])
            P.cp("vector", dd[:, 2048:4096], hTo[:, 0, 0:2048], reads=["hT"], writes=["dd"])
            P.ld("sync", dbg_o[:, :], dd, reads=["dd"])
        P.barrier()
        A.release(own_mark)


    y2T = A.alloc([4, TOWN + NDEC], BF16)
    y2_mark = A.mark()
    if "S" in phases and "B" in phases:
        TC = 256
        uT = A.alloc([4, TOWN + NDEC], BF16)
        C1 = A.alloc([16, TC], F32)
        S1 = A.alloc([16, TC], F32)
        m_t = A.mark()
        etmp = [A.alloc([16, 64], F32) for _ in range(4)]
        phasor_table(C1[:, :, 0:128], S1[:, :, 0:128], 128, 0, etmp, "CS1")
        P.barrier()
        A.release(m_t)
        Rre = A.alloc([TC], F32)
        Rim = A.alloc([TC], F32)
        Gre = A.alloc([TC], F32)
        Gim = A.alloc([TC], F32)
        q1 = A.alloc([TC], F32)
        q2 = A.alloc([TC], F32)
        for pr in range(16):
            prc = PWr[:, 7, pr:pr + 1].to_broadcast([128, 128])
            pic = PWi[:, 7, pr:pr + 1].to_broadcast([128, 128])
            P.tt("vector", Rre[:, 0:128], C1[:, pr, 0:128], prc, ALU.mult, reads=["CS1", "PW", "Rre"], writes=["Rre"])
            P.tt("vector", Rim[:, 0:128], S1[:, pr, 0:128], pic, ALU.mult, reads=["CS1", "PW", "Rim"], writes=["Rim"])
            P.tt("vector", Gre[:, 0:128], C1[:, pr, 0:128], pic, ALU.mult, reads=["CS1", "PW", "Gre"], writes=["Gre"])
            P.tt("vector", Gim[:, 0:128], S1[:, pr, 0:128], prc, ALU.mult, reads=["CS1", "PW", "Gim"], writes=["Gim"])
            P.tt("vector", C1[:, pr, 128:256], Rre[:, 0:128], Rim[:, 0:128], ALU.subtract, reads=["Rre", "Rim"], writes=["CS1"])
            P.tt("vector", S1[:, pr, 128:256], Gre[:, 0:128], Gim[:, 0:128], ALU.add, reads=["Gre", "Gim"], writes=["CS1"])
        cT = PWr[:, 8, :]
        sT = PWi[:, 8, :]
        H16 = [A.alloc([TC], BF16) for _ in range(2)]
        wtile_alloc()
        wgst = A.alloc([4, 512], F32)
        wg16 = A.alloc([4, 512], BF16)
        dsk = A.alloc([4], F32)
        bgl = A.alloc([4], F32)
        h0t = A.alloc([16, NDEC, 2], F32)
        hsd = A.alloc([16, NDEC, 2], F32)
        hsp = A.alloc([16, 2], F32)
        yv = A.alloc([4, TC], F32)
        yg = A.alloc([4, TC], F32)
        yg16 = A.alloc([4, TC], BF16)
        sg = A.alloc([TC], F32)
        P.ld("sync", wgst, w_glu[:, :, :], writes=["wgst"])
        P.cp("gpsimd", wg16, wgst, reads=["wgst"], writes=["wg16"])
        P.ld("sync", dsk, dskip_d[:, :], writes=["dsk"])
        P.ld("sync", bgl, bglu_d[:, :], writes=["bgl"])
        P.ld("sync", h0t, h0_d[:, :, :, :], writes=["h0t"])
        for ct in range(4):
            wu, kwu = load_wtile(TI[("u", ct)])
            for (c0, n) in chunks(TOWN, TALL):
                bank, bkey = nextbank()
                proj(wu, kwu, hT, "hT", c0, n, bank, bkey)
                P.cp("scalar", uT[:, ct, c0 - TOWN:c0 - TOWN + n], bank[:, 0:n], reads=[bkey], writes=["uT"])
        ybanks = banks[0:4]
        nchunk = TOWN // TC
        for kc in list(range(nchunk)) + ["dec"]:
            dec = kc == "dec"
            o0, n = (TOWN, NDEC) if dec else (kc * TC, TC)
            for ct in range(4):
                yb, ykey = ybanks[ct]
                for pm in range(4):
                    pr = ct * 4 + pm
                    bank, bkey = nextbank()
                    bur = bank[:, 0:n]
                    bui = bank[:, 256:256 + n]
                    P.mm(bur, WB[:, 0, pr, :], uT[:, ct, o0:o0 + n], reads=["WB", "uT"], writes=[bkey])
                    P.mm(bui, WB[:, 1, pr, :], uT[:, ct, o0:o0 + n], reads=["WB", "uT"], writes=[bkey])
                    hre16, him16 = H16[0][:, 0:n], H16[1][:, 0:n]
                    if not dec:
                        c1 = C1[:, pr, :]
                        s1 = S1[:, pr, :]
                        P.tt("vector", q1, bur, c1, ALU.mult, reads=[bkey, "CS1"], writes=["q1"])
                        P.tt("vector", q2, bui, s1, ALU.mult, reads=[bkey, "CS1"], writes=["q2"])
                        P.tt("gpsimd", Rre, q1, q2, ALU.add, reads=["q1", "q2"], writes=["Rre"])
                        P.tt("vector", q1, bui, c1, ALU.mult, reads=[bkey, "CS1", "Rre"], writes=["q1"])
                        P.tt("vector", q2, bur, s1, ALU.mult, reads=[bkey, "CS1", "Rre"], writes=["q2"])
                        P.tt("gpsimd", Rim, q1, q2, ALU.subtract, reads=["q1", "q2"], writes=["Rim"])
                        P.scan(Gre, r1[:, pr:pr + 1].to_broadcast([128, TC]), Rre, gi_re[:, pr:pr + 1], reads=["Rre", "r1", "gi"], writes=["Gre"])
                        P.scan(Gim, r1[:, pr:pr + 1].to_broadcast([128, TC]), Rim, gi_im[:, pr:pr + 1], reads=["Rim", "r1", "gi"], writes=["Gim"])
                        ta_ = tmpc[:, 0, pr:pr + 1]
                        tb_ = tmpc[:, 1, pr:pr + 1]
                        if kc == nchunk - 1:
                            cl, sl_ = C1[:, pr, TC - 1:TC], S1[:, pr, TC - 1:TC]
                            P.tt("vector", ta_, Gim[:, TC - 1:TC], sl_, ALU.mult, reads=["Gim", "CS1"], writes=["tmpc"])
                            P.tt("vector", tb_, Gre[:, TC - 1:TC], sl_, ALU.mult, reads=["Gre", "CS1"], writes=["tmpc"])
                            P.tt("vector", hsp[:, pr, 0:1], Gre[:, TC - 1:TC], cl, ALU.mult, reads=["Gre", "CS1"], writes=["hsp"])
                            P.tt("vector", hsp[:, pr, 0:1], hsp[:, pr, 0:1], ta_, ALU.subtract, reads=["hsp", "tmpc"], writes=["hsp"])
                            P.tt("vector", hsp[:, pr, 1:2], Gim[:, TC - 1:TC], cl, ALU.mult, reads=["Gim", "CS1"], writes=["hsp"])
                            P.tt("vector", hsp[:, pr, 1:2], hsp[:, pr, 1:2], tb_, ALU.add, reads=["hsp", "tmpc"], writes=["hsp"])
                        P.ts("vector", ta_, Gim[:, TC - 1:TC], sT[:, pr:pr + 1], None, ALU.mult, reads=["Gim", "PW"], writes=["tmpc"])
                        P.ts("vector", tb_, Gre[:, TC - 1:TC], sT[:, pr:pr + 1], None, ALU.mult, reads=["Gre", "PW"], writes=["tmpc"])
                        P.stt("vector", gi_re[:, pr:pr + 1], Gre[:, TC - 1:TC], cT[:, pr:pr + 1], ta_, ALU.mult, ALU.subtract, reads=["Gre", "PW", "tmpc"], writes=["gi"])
                        P.stt("vector", gi_im[:, pr:pr + 1], Gim[:, TC - 1:TC], cT[:, pr:pr + 1], tb_, ALU.mult, ALU.add, reads=["Gim", "PW", "tmpc"], writes=["gi"])
                        P.tt("vector", q1, Gre, c1, ALU.mult, reads=["Gre", "CS1"], writes=["q1"])
                        P.tt("vector", q2, Gim, s1, ALU.mult, reads=["Gim", "CS1"], writes=["q2"])
                        P.tt("gpsimd", hre16, q1, q2, ALU.subtract, reads=["q1", "q2"], writes=["H16r"])
                        P.tt("vector", q1, Gim, c1, ALU.mult, reads=["Gim", "CS1", "H16r"], writes=["q1"])
                        P.tt("vector", q2, Gre, s1, ALU.mult, reads=["Gre", "CS1", "H16r"], writes=["q2"])
                        P.tt("gpsimd", him16, q1, q2, ALU.add, reads=["q1", "q2"], writes=["H16i"])
                    else:
                        h0r, h0i = h0t[:, pr, :, 0], h0t[:, pr, :, 1]
                        hnr, hni = hsd[:, pr, :, 0], hsd[:, pr, :, 1]
                        qa, qb_ = q1[:, 0:n], q2[:, 0:n]
                        lr, li = lb_re[:, pr:pr + 1], lb_im[:, pr:pr + 1]
                        P.ts("vector", qa, h0i, li, None, ALU.mult, reads=["h0t", "lb_im"], writes=["q1"])
                        P.stt("vector", qa, h0r, lr, qa, ALU.mult, ALU.subtract, reads=["h0t", "lb_re", "q1"], writes=["q1"])
                        P.tt("vector", hnr, qa, bur, ALU.add, reads=["q1", bkey], writes=["hsd"])
                        P.ts("vector", qb_, h0r, li, None, ALU.mult, reads=["h0t", "lb_im"], writes=["q2"])
                        P.stt("vector", qb_, h0i, lr, qb_, ALU.mult, ALU.add, reads=["h0t", "lb_re", "q2"], writes=["q2"])
                        P.tt("vector", hni, qb_, bui, ALU.add, reads=["q2", bkey], writes=["hsd"])
                        P.cp("gpsimd", hre16, hnr, reads=["hsd"], writes=["H16r"])
                        P.cp("gpsimd", him16, hni, reads=["hsd"], writes=["H16i"])
                    P.mm(yb[:, 0:n], WC[:, 0, pr, :], hre16, start=(pm == 0), stop=False, reads=["WC", "H16r"], writes=[ykey])
                    P.mm(yb[:, 0:n], WC[:, 1, pr, :], him16, start=False, stop=(pm == 3), reads=["WC", "H16i"], writes=[ykey])
                P.stt("vector", yv[:, ct, 0:n], uT[:, ct, o0:o0 + n], dsk[:, ct:ct + 1], yb[:, 0:n], ALU.mult, ALU.add, reads=["uT", "dsk", ykey], writes=["yv"])
                P.act(yg[:, ct, 0:n], yv[:, ct, 0:n], AF.Gelu, reads=["yv"], writes=["yg"])
                P.cp("gpsimd", yg16[:, ct, 0:n], yg[:, ct, 0:n], reads=["yg"], writes=["yg16"])
            for ct2 in range(4):
                bank, bkey = nextbank()
                for ct in range(4):
                    P.mm(bank[:, 0:n], wg16[:, ct, ct2 * 128:(ct2 + 1) * 128], yg16[:, ct, 0:n], start=(ct == 0), stop=(ct == 3), reads=["wg16", "yg16"], writes=[bkey])
                P.act(sg[:, 0:n], bank[:, 0:n], AF.Sigmoid, reads=[bkey, "bgl"], writes=["sg"], bias=bgl[:, ct2:ct2 + 1])
                P.tt("vector", y2T[:, ct2, o0:o0 + n], yg[:, ct2, 0:n], sg[:, 0:n], ALU.mult, reads=["yg", "sg"], writes=["y2T"])
        P.ld("sync", ssmp_o[:, :, :], hsp, reads=["hsp"])
        P.ld("sync", ssms_o[:, :, :, :], hsd, reads=["hsd"])
        if dbg == "S":
            P.barrier()
            A.release(y2_mark)
            dd = A.alloc([4096], F32)
            P.memset("gpsimd", dd, 0.0, writes=["dd"])
            P.cp("vector", dd[:, 0:2064], y2T[:, 0, :], reads=["y2T"], writes=["dd"])
            P.cp("vector", dd[:, 2064:2064 + 2032], y2T[:, 3, 0:2032], reads=["y2T"], writes=["dd"])
            P.ld("sync", dbg_o[:, :], dd, reads=["dd"])
        P.barrier()
        A.release(y2_mark)


    if "C" in phases:
        zT = A.alloc([8, TOWN + NDEC], BF16)
        wo16 = A.alloc([8, 1024], BF16)
        wost = A.alloc([1024], F32)
        for kt in range(8):
            P.ld("sync", wost, w_out[:, kt, :], writes=["wost"])
            P.cp("gpsimd", wo16[:, kt, :], wost, reads=["wost"], writes=["wo16"])
        wtile_alloc()
        wbst = [A.alloc([4, 128], F32) for _ in range(2)]
        wb16 = [A.alloc([4, 128], BF16) for _ in range(2)]
        sgb = [A.alloc([512], F32) for _ in range(2)]
        tzb = [A.alloc([512], F32) for _ in range(2)]
        for d in range(8):
            wgs, kgs = load_wtile(TI[("gs", d)])
            wga, kga = load_wtile(TI[("ga", d)])
            for i, src in enumerate((w_bs, w_ba)):
                P.ld("sync", wbst[i], src[:, :, d * 128:(d + 1) * 128], writes=[("wbst", i)])
                P.cp("gpsimd", wb16[i], wbst[i], reads=[("wbst", i)], writes=[("wb16", i)])
            for (c0, n) in chunks(TOWN, TALL):
                o0 = c0 - TOWN
                for i, (wg, kg, src, skey) in enumerate(((wgs, kgs, y2T, "y2T"), (wga, kga, attnT, "attnT"))):
                    b1, k1 = nextbank()
                    proj(wg, kg, hT, "hT", c0, n, b1, k1)
                    P.act(sgb[i][:, 0:n], b1[:, 0:n], AF.Sigmoid, reads=[k1], writes=[("sgb", i)])
                    b2, k2 = nextbank()
                    for ct in range(4):
                        P.mm(b2[:, 0:n], wb16[i][:, ct, :], src[:, ct, o0:o0 + n], start=(ct == 0), stop=(ct == 3), reads=[("wb16", i), skey], writes=[k2])
                    P.tt("vector", tzb[i][:, 0:n], b2[:, 0:n], sgb[i][:, 0:n], ALU.mult, reads=[k2, ("sgb", i)], writes=[("tzb", i)])
                P.tt("gpsimd", zT[:, d, o0:o0 + n], tzb[0][:, 0:n], tzb[1][:, 0:n], ALU.add, reads=[("tzb", 0), ("tzb", 1)], writes=["zT"])
        xcb = [A.alloc([1024], F32) for _ in range(2)]
        x1b = [A.alloc([1024], F32) for _ in range(2)]
        for t in range(17):
            ntok = 128 if t < 16 else NDEC
            o0 = t * 128
            par = t % 2
            src = x_all[TPRI + o0:TPRI + o0 + 128, :] if t < 16 else x_dec[:, :]
            P.ld("sync", xcb[par][0:ntok, :], src, writes=[("xcb", par)])
            for hh in range(2):
                bank, bkey = nextbank()
                for kt in range(8):
                    P.mm(bank[0:ntok, 0:512], zT[:, kt, o0:o0 + ntok], wo16[:, kt, hh * 512:(hh + 1) * 512], start=(kt == 0), stop=(kt == 7),
                         reads=["zT", "wo16"], writes=[bkey])
                P.tt("vector", x1b[par][0:ntok, hh * 512:(hh + 1) * 512], bank[0:ntok, 0:512], xcb[par][0:ntok, hh * 512:(hh + 1) * 512], ALU.add,
                     reads=[bkey, ("xcb", par)], writes=[("x1b", par)])
            P.ld("sync", x1_scr[o0:o0 + ntok, :], x1b[par][0:ntok, :], reads=[("x1b", par)], writes=["x1_scr"])
        if dbg == "C":
            dd = A.alloc([4096], F32)
            P.memset("gpsimd", dd, 0.0, writes=["dd"])
            P.cp("vector", dd[:, 0:2064], zT[:, 0, :], reads=["zT"], writes=["dd"])
            P.cp("vector", dd[:, 3072:4096], x1b[1][:, :], reads=[("x1b", 1)], writes=["dd"])
            P.ld("sync", dbg_o[:, :], dd, reads=["dd"])
    P.barrier()
    A.release(base_mark)

    if "D" in phases:
        wup16 = A.alloc([8, 4096], BF16)
        wdn16 = A.alloc([32, 1024], BF16)
        wst2 = [A.alloc([1024], F32) for _ in range(2)]
        gfin = A.alloc([1024], F32)
        P.ld("sync", gfin, gfin_d[:, :], writes=["gfin"])
        li = 0
        for kt in range(8):
            for q in range(4):
                par = li % 2
                li += 1
                P.ld("sync", wst2[par], w_up[:, kt, q * 1024:(q + 1) * 1024], writes=[("wst2", par)])
                P.ts("gpsimd" if par else "vector", wup16[:, kt, q * 1024:(q + 1) * 1024], wst2[par], gmlp[:, kt:kt + 1], None, ALU.mult,
                     reads=[("wst2", par), "gmlp"], writes=["wup16"])
        for f in range(32):
            par = li % 2
            li += 1
            P.ld("sync", wst2[par], w_down[:, f, :], writes=[("wst2", par)])
            P.cp("gpsimd" if par else "vector", wdn16[:, f, :], wst2[par], reads=[("wst2", par)], writes=["wdn16"])
        frontend_alloc()
        x1k = [A.alloc([1024], F32) for _ in range(2)]
        hmT = A.alloc([8, 256], BF16)
        a16 = [A.alloc([256], BF16) for _ in range(2)]
        r32 = [A.alloc([256], F32) for _ in range(2)]
        x2 = A.alloc([1024], F32)
        yo = A.alloc([1024], F32)
        st2 = A.alloc([2], F32)
        for ch in range(9):
            dec = ch == 8
            tiles = [(ch * 256 + j * 128, 128) for j in range(2)] if not dec else [(TOWN, NDEC)]
            ncol = sum(t[1] for t in tiles)
            for j, (o0, ntok) in enumerate(tiles):
                frontend(x1_scr[o0:o0 + ntok, :], ntok, hmT, "hmT", j * 128, keep_x=x1k[j])
            for f in range(32):
                ub, ukey = nextbank()
                for kt in range(8):
                    P.mm(ub[:, 0:ncol], wup16[:, kt, f * 128:(f + 1) * 128], hmT[:, kt, 0:ncol], start=(kt == 0), stop=(kt == 7),
                         reads=["wup16", "hmT"], writes=[ukey])
                par = f % 2
                P.ts("vector", r32[par][:, 0:ncol], ub[:, 0:ncol], 0.0, None, ALU.max, reads=[ukey], writes=[("r32", par)])
                P.tt("gpsimd", a16[par][:, 0:ncol], r32[par][:, 0:ncol], r32[par][:, 0:ncol], ALU.mult, reads=[("r32", par)], writes=[("a16", par)])
                for j, (o0, ntok) in enumerate(tiles):
                    for hh in range(2):
                        ab, akey = banks[2 * j + hh]
                        P.mm(ab[0:ntok, 0:512], a16[par][:, j * 128:j * 128 + ntok], wdn16[:, f, hh * 512:(hh + 1) * 512], start=(f == 0), stop=(f == 31),
                             reads=[("a16", par), "wdn16"], writes=[akey])
            for j, (o0, ntok) in enumerate(tiles):
                for hh in range(2):
                    ab, akey = banks[2 * j + hh]
                    P.tt("vector", x2[0:ntok, hh * 512:(hh + 1) * 512], ab[0:ntok, 0:512], x1k[j][0:ntok, hh * 512:(hh + 1) * 512], ALU.add,
                         reads=[akey, ("xk", id(x1k[j]))], writes=["x2"])
                P.memset("gpsimd", st2[0:ntok, 0:1], 0.0, writes=["st2"])
                P.act(junk[0][0:ntok, :], x2[0:ntok, :], AF.Square, reads=["x2", "st2"], writes=["junk", "st2"], scale=1.0 / 32.0, accum_out=st2[0:ntok, 0:1])
                P.act(st2[0:ntok, 1:2], st2[0:ntok, 0:1], AF.Sqrt, reads=["st2", "epscol"], writes=["st2"], bias=epscol[0:ntok, 0:1])
                P.recip(st2[0:ntok, 1:2], st2[0:ntok, 1:2], reads=["st2"], writes=["st2"])
                P.stt("vector", yo[0:ntok, :], x2[0:ntok, :], st2[0:ntok, 1:2], gfin[0:ntok, :], ALU.mult, ALU.mult, reads=["x2", "st2", "gfin"], writes=["yo"])
                if dec:
                    P.ld("sync", yd_o[:, :], yo[0:ntok, :], reads=["yo"])
                else:
                    P.ld("sync", y_o[o0:o0 + ntok, :], yo[0:ntok, :], reads=["yo"])

    P.build()
    return nc, P


def _rope_tables(core):
    half = 8
    inv_freq = np.exp(np.float32(-math.log(500000.0)) * np.arange(half, dtype=np.float32) * np.float32(2.0 / 16)).astype(np.float32)
    pos = np.concatenate([2048 * (core - 1) + np.arange(TEXT), np.full(NDEC, 8192)]).astype(np.float32)
    ang = pos[:, None] * inv_freq[None, :]
    cos = np.cos(ang).astype(np.float32)
    sin = np.sin(ang).astype(np.float32)
    C = np.ones((128, TALL), np.float32)
    S = np.zeros((128, TALL), np.float32)
    for j in range(128):
        e = j % 64
        if e < 8:
            C[j] = cos[:, e]
            S[j] = -sin[:, e]
        elif e < 16:
            C[j] = cos[:, e - 8]
            S[j] = sin[:, e - 8]
    tm = np.zeros((128, 17, 16), np.float32)
    for t in range(16):
        sl = slice(2048 + t * 128, 2048 + (t + 1) * 128)
        tm[:, t, 0:8] = cos[sl]
        tm[:, t, 8:16] = sin[sl]
    tm[0:NDEC, 16, 0:8] = cos[TEXT:TEXT + NDEC]
    tm[0:NDEC, 16, 8:16] = sin[TEXT:TEXT + NDEC]
    return C, S, tm


def _prep_shared(inp):
    w_in = np.asarray(inp["w_in"][0], np.float32)
    sh = {}
    wt = np.empty((NT, 128, 8, 128), np.float32)
    for i, cols in enumerate(TCOLS):
        wt[i] = w_in[:, cols].reshape(8, 128, 128).transpose(1, 0, 2)
    sh["w_tiles"] = wt
    wkv = np.empty((6, 128, 8, 512), np.float32)
    for kv in range(2):
        for g in range(3):
            c0 = 1536 * (kv + 1) + g * 512
            wkv[kv * 3 + g] = w_in[:, c0:c0 + 512].reshape(8, 128, 512).transpose(1, 0, 2)
    sh["w_kv"] = wkv

    def kmaj(w, kt):
        return np.ascontiguousarray(np.asarray(w, np.float32).reshape(kt, 128, -1).transpose(1, 0, 2))

    sh["w_glu"] = kmaj(inp["w_glu"][0], 4)
    sh["w_bs"] = kmaj(inp["w_branch_ssm"][0], 4)
    sh["w_ba"] = kmaj(inp["w_branch_attn"][0], 4)
    sh["w_out"] = kmaj(inp["w_out"][0], 8)
    sh["w_up"] = kmaj(inp["w_up"][0], 8)
    sh["w_down"] = kmaj(inp["w_down"][0], 32)
    sh["gm"] = np.ascontiguousarray(np.asarray(inp["norm_mix"][0], np.float32).reshape(8, 128).T)
    sh["gmlp"] = np.ascontiguousarray(np.asarray(inp["norm_mlp"][0], np.float32).reshape(8, 128).T)
    sh["gfin"] = np.ascontiguousarray(np.broadcast_to(np.asarray(inp["norm_final"], np.float32)[None, :], (128, 1024)))

    def st(a):
        return np.asarray(a, np.float32).reshape(16, 2, 64).transpose(1, 2, 0).reshape(128, 16)

    logdt = np.broadcast_to(np.asarray(inp["ssm_log_dt"][0], np.float32)[:, None], (32, 64))
    sh["ssm_small"] = np.ascontiguousarray(np.stack([st(inp["ssm_lambda_re"][0]), st(inp["ssm_lambda_im"][0]), st(logdt)], axis=-1))

    def stb(a):
        return np.ascontiguousarray(np.asarray(a, np.float32).reshape(16, 2, 64, 16).transpose(1, 2, 0, 3).reshape(128, 16, 16))

    sh["b_re"] = stb(inp["ssm_b_re"][0])
    sh["b_im"] = stb(inp["ssm_b_im"][0])

    def stc(a):
        a = np.asarray(a, np.float32)
        o = np.zeros((128, 16, 128), np.float32)
        for pr in range(16):
            pm = pr % 4
            for gl in range(2):
                o[64 * gl:64 * gl + 64, pr, 32 * pm + 16 * gl:32 * pm + 16 * gl + 16] = a[2 * pr + gl].T
        return o

    sh["wc_re"] = stc(inp["ssm_c_re"][0])
    sh["wc_im"] = stc(inp["ssm_c_im"][0])
    sh["dskip"] = np.ascontiguousarray(np.asarray(inp["ssm_d"][0], np.float32).reshape(4, 128).T)
    sh["bglu"] = np.ascontiguousarray(np.asarray(inp["b_glu"][0], np.float32).reshape(4, 128).T)
    b = np.arange(128)[:, None]
    a = np.arange(128)[None, :]
    m2 = np.zeros((128, 256), np.float32)
    m2[:, 0:128] = np.where(b <= a, 0.0, NEG)
    m2[:, 128:256] = np.where(b >= a, 0.0, NEG)
    sh["mask2"] = m2
    em = np.zeros((8, 512), np.float32)
    for h in range(8):
        em[h, h * 64:(h + 1) * 64] = 1.0
    sh["em"] = em
    bo = np.zeros((128, 128), np.float32)
    bo[0:64, 0:64] = 1.0
    bo[64:128, 64:128] = 1.0
    sh["bones"] = bo
    return sh


def _prep_core(inp, sh, c):
    m = dict(sh)
    xp = np.asarray(inp["x_prompt"][0], np.float32)
    xa = np.zeros((TPRI + TOWN, 1024), np.float32)
    n = 2048 * c + 2048
    xa[TPRI + TOWN - n:] = xp[0:n]
    m["x_all"] = xa
    m["x_dec"] = np.ascontiguousarray(np.asarray(inp["x_sample"], np.float32)[16 * c:16 * c + 16, 0, :])
    stt = np.asarray(inp["state_ssm"][0], np.float32)[16 * c:16 * c + 16]
    m["h0"] = np.ascontiguousarray(stt.reshape(16, 16, 2, 64, 2).transpose(2, 3, 1, 0, 4).reshape(128, 16, 16, 2))
    C, S, tm = _rope_tables(c)
    m["rope_c"], m["rope_s"], m["rope_tm"] = C, S, tm
    m["maskh"] = sh["mask2"][:, 128:256].copy() if c > 0 else np.full((128, 128), NEG, np.float32)
    for g, nm in enumerate(["cache_kv_g1", "cache_kv_g2", "cache_kv_g3"]):
        cc = np.asarray(inp[nm][0], np.float32)[16 * c:16 * c + 16]
        m["cache%d" % (g + 1)] = np.ascontiguousarray(cc.reshape(16, cc.shape[1], 2, 512))
    return m


def kernel(**inp):
    nc, _ = build_program("ABSCDEFG")
    sh = _prep_shared(inp)
    in_maps = [_prep_core(inp, sh, c) for c in range(NCORE)]
    res = run_bass_kernel_spmd(nc, in_maps, core_ids=list(range(NCORE)))
    r = res.results
    y_prompt = np.concatenate([r[c]["y"] for c in range(NCORE)], 0)[None]
    y_sample = np.concatenate([r[c]["y_dec"] for c in range(NCORE)], 0)[:, None, :]
    kvp = [r[NCORE - 1]["kvp%d" % (g + 1)].reshape(1, 1, -1, 2, 8, 64) for g in range(3)]
    sp = r[NCORE - 1]["ssmp"].reshape(2, 64, 16, 2).transpose(2, 0, 1, 3).reshape(1, 1, 32, 64, 2)
    kvs = [np.concatenate([r[c]["kvs%d" % (g + 1)] for c in range(NCORE)], 0).reshape(1, 128, -1, 2, 8, 64) for g in range(3)]
    ss = np.concatenate([r[c]["ssms"].reshape(2, 64, 16, 16, 2).transpose(3, 2, 0, 1, 4).reshape(16, 32, 64, 2) for c in range(NCORE)], 0)[None]
    return (y_prompt, y_sample, kvp[0], kvp[1], kvp[2], sp, kvs[0], kvs[1], kvs[2], ss)
```

```python
import math
import numpy as np
import concourse.bass as bass
import concourse.mybir as mybir
from concourse.bass_utils import run_bass_kernel_spmd

F32 = mybir.dt.float32
BF16 = mybir.dt.bfloat16
U8 = mybir.dt.uint8
ALU = mybir.AluOpType
AF = mybir.ActivationFunctionType
AX = mybir.AxisListType

import os
DBG_STAGE = int(os.environ.get('KSTAGE', '9'))
DBG_NCH = int(os.environ.get('KNCH', '14'))
DBG_FE = int(os.environ.get('KFE', '9'))
DBG_NT = int(os.environ.get('KNT', '8'))
DBG_SKIP = int(os.environ.get('KSKIP', '0'))
NCORE = 8
TOWN = 2048
TEXT = 4096
NDEC = 16
TALL = TEXT + NDEC
TPRI = 14336
PI = math.pi
NEG = -30000.0
COMPUTE = ("tensor", "vector", "scalar", "gpsimd")
SAME_ENGINE_SYNC = ("vector", "scalar", "gpsimd")
DMA_SLOTS = {"sync": 16, "gpsimd": 8, "scalar": 56}


class Prog:
    def __init__(self, nc):
        self.nc = nc
        self.ops = []
        self.bar = 0

    def op(self, eng, fn, reads=(), writes=()):
        self.ops.append(dict(eng=eng, fn=fn, reads=tuple(reads), writes=tuple(writes), dma=False, bar=self.bar))

    def dma(self, queue, fn, reads=(), writes=()):
        self.ops.append(dict(eng=queue, fn=fn, reads=tuple(reads), writes=tuple(writes), dma=True, bar=self.bar))

    def barrier(self):
        self.bar += 1

    def mm(self, out, lhsT, rhs, start=True, stop=True, reads=(), writes=()):
        self.op("tensor", lambda e: e.matmul(out, lhsT=lhsT, rhs=rhs, start=start, stop=stop, skip_group_check=True), reads, writes)

    def tr(self, out, in_, ident, reads=(), writes=()):
        self.op("tensor", lambda e: e.transpose(out, in_, ident), reads, writes)

    def act(self, out, in_, func, reads=(), writes=(), **kw):
        self.op("scalar", lambda e: e.activation(out=out, in_=in_, func=func, **kw), reads, writes)

    def tt(self, eng, out, in0, in1, op, reads=(), writes=()):
        self.op(eng, lambda e: e.tensor_tensor(out=out, in0=in0, in1=in1, op=op), reads, writes)

    def ts(self, eng, out, in0, s1, s2, op0, op1=None, reads=(), writes=()):
        if op1 is None:
            self.op(eng, lambda e: e.tensor_scalar(out=out, in0=in0, scalar1=s1, scalar2=s2, op0=op0), reads, writes)
        else:
            self.op(eng, lambda e: e.tensor_scalar(out=out, in0=in0, scalar1=s1, scalar2=s2, op0=op0, op1=op1), reads, writes)

    def stt(self, eng, out, in0, scalar, in1, op0, op1, reads=(), writes=()):
        self.op(eng, lambda e: e.scalar_tensor_tensor(out=out, in0=in0, scalar=scalar, in1=in1, op0=op0, op1=op1), reads, writes)

    def cp(self, eng, out, in_, reads=(), writes=()):
        if eng == "scalar":
            self.op(eng, lambda e: e.activation(out=out, in_=in_, func=AF.Copy), reads, writes)
        else:
            self.op(eng, lambda e: e.tensor_copy(out=out, in_=in_), reads, writes)

    def memset(self, eng, ap, val, writes=()):
        self.op(eng, lambda e: e.memset(ap, val), (), writes)

    def scan(self, out, d0, d1, initial, reads=(), writes=()):
        self.op("vector", lambda e: e.tensor_tensor_scan(out=out, data0=d0, data1=d1, initial=initial, op0=ALU.mult, op1=ALU.add), reads, writes)

    def red(self, eng, out, in_, op, axis, reads=(), writes=()):
        self.op(eng, lambda e: e.tensor_reduce(out=out, in_=in_, axis=axis, op=op), reads, writes)

    def recip(self, out, in_, reads=(), writes=()):
        self.op("vector", lambda e: e.reciprocal(out=out, in_=in_), reads, writes)

    def ld(self, q, out, in_, reads=(), writes=()):
        self.dma(q, lambda e: e.dma_start(out=out, in_=in_), reads, writes)

    def build(self):
        nc = self.nc
        ops = self.ops
        n = len(ops)
        last_writer = {}
        readers = {}
        deps = [set() for _ in range(n)]
        for i, o in enumerate(ops):
            for r in o["reads"]:
                w = last_writer.get(r)
                if w is not None:
                    deps[i].add(w)
            for r in o["writes"]:
                w = last_writer.get(r)
                if w is not None:
                    deps[i].add(w)
                for rd in readers.get(r, ()):
                    if rd != i:
                        deps[i].add(rd)
            for r in o["reads"]:
                readers.setdefault(r, []).append(i)
            for r in o["writes"]:
                last_writer[r] = i
                readers[r] = []
        nb = self.bar
        if nb:
            last_of = {}
            dmas_before = []
            seen_first = set()
            cur = 0
            snapshot = None
            for i, o in enumerate(ops):
                if o["bar"] != cur:
                    cur = o["bar"]
                    snapshot = (dict(last_of), list(dmas_before))
                    seen_first = set()
                if snapshot is not None and o["eng"] not in seen_first:
                    seen_first.add(o["eng"])
                    deps[i].update(snapshot[0].values())
                    deps[i].update(snapshot[1])
                if o["dma"]:
                    dmas_before.append(i)
                else:
                    last_of[o["eng"]] = i
        for i, o in enumerate(ops):
            keep = set()
            best = {}
            for d in deps[i]:
                od = ops[d]
                if od["dma"]:
                    keep.add(d)
                    continue
                if od["eng"] == o["eng"] and od["eng"] not in SAME_ENGINE_SYNC:
                    continue
                e = od["eng"]
                if e not in best or d > best[e]:
                    best[e] = d
            keep.update(best.values())
            deps[i] = keep
        needed = set()
        for i in range(n):
            needed.update(deps[i])
        tick = {}
        eng_count = {e: 0 for e in COMPUTE}
        dma_queues = sorted({o["eng"] for o in ops if o["dma"]})
        dma_count = {q: 0 for q in dma_queues}
        dma_slot = {}
        for i, o in enumerate(ops):
            if o["dma"]:
                q = o["eng"]
                k = dma_count[q]
                dma_count[q] += 1
                ns = DMA_SLOTS[q]
                dma_slot[i] = (q, k % ns, k // ns + 1)
            elif i in needed:
                eng_count[o["eng"]] += 1
                tick[i] = eng_count[o["eng"]]
        sems = {e: nc.alloc_semaphore("s_" + e) for e in COMPUTE}
        dsems = {}
        for q in dma_queues:
            for s in range(min(DMA_SLOTS[q], dma_count[q])):
                dsems[(q, s)] = nc.alloc_semaphore("d_%s_%d" % (q, s))
        by_eng = {}
        for i, o in enumerate(ops):
            by_eng.setdefault(o["eng"], []).append(i)
        self.stats = {e: len(v) for e, v in by_eng.items()}

        def emit(engname, eng):
            waited = {}

            def wait(key, sem, val):
                if waited.get(key, 0) >= val:
                    return
                waited[key] = val
                eng.wait_ge(sem, val)

            for i in by_eng.get(engname, []):
                o = ops[i]
                for d in sorted(deps[i]):
                    od = ops[d]
                    if od["dma"]:
                        q, slot, use = dma_slot[d]
                        wait(("d", q, slot), dsems[(q, slot)], 16 * use)
                    else:
                        wait(("e", od["eng"]), sems[od["eng"]], tick[d])
                if o["dma"]:
                    q, slot, use = dma_slot[i]
                    if use > 1:
                        wait(("d", q, slot), dsems[(q, slot)], 16 * (use - 1))
                    o["fn"](eng).then_inc(dsems[(q, slot)], 16)
                else:
                    ins = o["fn"](eng)
                    if i in tick:
                        ins.then_inc(sems[o["eng"]], 1)
            if engname == "sync":
                last = {}
                for i in sorted(dma_slot):
                    q, slot, use = dma_slot[i]
                    last[(q, slot)] = use
                for (q, slot), use in last.items():
                    wait(("d", q, slot), dsems[(q, slot)], 16 * use)

        with nc.Block() as block:
            @block.sync
            def _(e):
                emit("sync", e)

            @block.tensor
            def _(e):
                emit("tensor", e)

            @block.vector
            def _(e):
                emit("vector", e)

            @block.scalar
            def _(e):
                emit("scalar", e)

            @block.gpsimd
            def _(e):
                emit("gpsimd", e)


class Arena:
    def __init__(self, nc, nbytes):
        self.t = nc.alloc_sbuf_tensor("arena", [128, nbytes], U8)
        self.nbytes = nbytes
        self.top = 0

    def alloc(self, shape, dtype):
        esz = 4 if dtype == F32 else 2
        n = 1
        for s in shape:
            n *= s
        nb = (n * esz + 31) // 32 * 32
        assert self.top + nb <= self.nbytes, "arena overflow %d + %d > %d" % (self.top, nb, self.nbytes)
        ap = self.t[:, self.top:self.top + n * esz].bitcast(dtype)
        self.top += nb
        if len(shape) == 2:
            ap = ap.rearrange("p (a b) -> p a b", a=shape[0])
        elif len(shape) == 3:
            ap = ap.rearrange("p (a b c) -> p a b c", a=shape[0], b=shape[1])
        return ap

    def mark(self):
        return self.top

    def release(self, m):
        self.top = m


def _tile_index():
    TI = {}
    cols = []

    def add(name, c):
        TI[name] = len(cols)
        cols.append(np.asarray(c, np.int64))

    j = np.arange(128)
    e = j % 64
    partner = np.where(e < 8, j + 8, np.where(e < 16, j - 8, j))
    for g in range(3):
        for hp in range(4):
            base = g * 512 + hp * 128
            add(("q", g, hp), base + j)
            add(("qp", g, hp), base + partner)
            add(("k", g, hp), 1536 + base + j)
            add(("kp", g, hp), 1536 + base + partner)
            add(("v", g, hp), 3072 + base + j)
    for ct in range(4):
        add(("u", ct), 4608 + ct * 128 + j)
    for d in range(8):
        add(("gs", d), 5120 + d * 128 + j)
        add(("ga", d), 6144 + d * 128 + j)
    return TI, cols


TI, TCOLS = _tile_index()
NT = len(TCOLS)
KSTART = (1920, 1536, 0)


def build_program(phases="ABCDEFG", dbg=False):
    nc = bass.Bass("TRN2", target_bir_lowering=False)
    P = Prog(nc)

    def din(name, shape):
        return nc.dram_tensor(name, list(shape), F32, kind="ExternalInput").ap()

    def dout(name, shape):
        return nc.dram_tensor(name, list(shape), F32, kind="ExternalOutput").ap()

    x_all = din("x_all", [TPRI + TOWN, 1024])
    x_dec = din("x_dec", [NDEC, 1024])
    w_tiles = din("w_tiles", [NT, 128, 8, 128])
    w_kv = din("w_kv", [6, 128, 8, 512])
    w_glu = din("w_glu", [128, 4, 512])
    w_bs = din("w_bs", [128, 4, 1024])
    w_ba = din("w_ba", [128, 4, 1024])
    w_out = din("w_out", [128, 8, 1024])
    w_up = din("w_up", [128, 8, 4096])
    w_down = din("w_down", [128, 32, 1024])
    gm_d = din("gm", [128, 8])
    gmlp_d = din("gmlp", [128, 8])
    gfin_d = din("gfin", [128, 1024])
    ssm_small = din("ssm_small", [128, 16, 3])
    b_re_d = din("b_re", [128, 16, 16])
    b_im_d = din("b_im", [128, 16, 16])
    wc_re_d = din("wc_re", [128, 16, 128])
    wc_im_d = din("wc_im", [128, 16, 128])
    dskip_d = din("dskip", [128, 4])
    bglu_d = din("bglu", [128, 4])
    h0_d = din("h0", [128, 16, 16, 2])
    rope_c_d = din("rope_c", [128, TALL])
    rope_s_d = din("rope_s", [128, TALL])
    rope_tm_d = din("rope_tm", [128, 17, 16])
    mask2_d = din("mask2", [128, 256])
    maskh_d = din("maskh", [128, 128])
    em_d = din("em", [8, 512])
    bones_d = din("bones", [128, 128])
    caches = [din("cache1", [NDEC, 128, 2, 512]), din("cache2", [NDEC, 512, 2, 512]), din("cache3", [NDEC, 2048, 2, 512])] if ("G" in phases or "F" in phases) else None
    LC = (128, 512, 2048)
    DIL = (1, 4, 16)

    y_o = dout("y", [TOWN, 1024])
    yd_o = dout("y_dec", [NDEC, 1024])
    kvp = [dout("kvp1", [128, 2, 512]), dout("kvp2", [512, 2, 512]), dout("kvp3", [2048, 2, 512])]
    ssmp_o = dout("ssmp", [128, 16, 2])
    kvs = [dout("kvs1", [NDEC, 128, 2, 512]), dout("kvs2", [NDEC, 512, 2, 512]), dout("kvs3", [NDEC, 2048, 2, 512])] if ("G" in phases or "E" in phases) else None
    ssms_o = dout("ssms", [128, 16, NDEC, 2])
    x1_scr = nc.dram_tensor("x1_scr", [TOWN + NDEC, 1024], F32).ap()
    dbg_o = dout("dbg", [128, 4096]) if dbg else None

    pacc = nc.alloc_psum_tensor("pacc", [128, 2048], F32)
    pws = [nc.alloc_psum_tensor("pw%d" % i, [128, 512], F32) for i in range(3)]
    ptr_t = nc.alloc_psum_tensor("ptr", [128, 1024], BF16)
    banks = [(pacc[:, 512 * k:512 * (k + 1)], "pacc%d" % k) for k in range(4)] + [(pws[i][:, :], "pw%d" % i) for i in range(3)]
    gen_banks = banks[4:]
    rr = {"i": 0}

    def nextbank(pool=None):
        pool = pool or gen_banks
        b = pool[rr["i"] % len(pool)]
        rr["i"] += 1
        return b

    A = Arena(nc, 204 * 1024)
    ident16 = A.alloc([128], BF16)
    ident32 = A.alloc([128], F32)
    zeros16 = A.alloc([128], BF16)
    ones32 = A.alloc([1], F32)
    picol = A.alloc([1], F32)
    epscol = A.alloc([1], F32)
    gm = A.alloc([8], F32)
    gmlp = A.alloc([8], F32)
    stat = A.alloc([8], F32)
    P.memset("gpsimd", ident16, 1.0, writes=["ident16"])
    P.op("gpsimd", lambda e: e.affine_select(out=ident16, in_=ident16, pattern=[[-1, 128]], compare_op=ALU.is_equal,
                                             fill=0.0, base=0, channel_multiplier=1), ["ident16"], ["ident16"])
    P.cp("vector", ident32, ident16, ["ident16"], ["ident32"])
    P.memset("gpsimd", zeros16, 0.0, writes=["zeros16"])
    P.memset("gpsimd", ones32, 1.0, writes=["ones32"])
    P.memset("gpsimd", picol, PI, writes=["picol"])
    P.memset("gpsimd", epscol, 1e-6, writes=["epscol"])
    P.ld("sync", gm, gm_d[:, :], writes=["gm"])
    P.ld("sync", gmlp, gmlp_d[:, :], writes=["gmlp"])

    th = A.alloc([16], F32)
    are = A.alloc([16], F32)
    r1 = A.alloc([16], F32)
    gi_re = A.alloc([16], F32)
    gi_im = A.alloc([16], F32)
    lb_re = A.alloc([16], F32)
    lb_im = A.alloc([16], F32)
    WB = A.alloc([2, 16, 128], BF16)
    WC = A.alloc([2, 16, 128], BF16)
    tmpc = A.alloc([8, 16], F32)
    NPW = 11
    PWr = A.alloc([NPW, 16], F32)
    PWi = A.alloc([NPW, 16], F32)
    base_mark = A.mark()

    if "G" in phases:
        for g in range(3):
            lc = LC[g]
            for b in range(NDEC):
                P.ld("scalar", kvs[g][b, 0:lc - 1, :, :], caches[g][b, 1:lc, :, :], writes=[("kvs", g, b)])

    xb = [None, None]
    h16b = [None, None]
    junk = [None]
    fe_cnt = {"i": 0}

    def frontend_alloc():
        xb[0] = A.alloc([1024], F32)
        xb[1] = A.alloc([1024], F32)
        h16b[0] = A.alloc([1024], BF16)
        h16b[1] = A.alloc([1024], BF16)
        junk[0] = A.alloc([1024], BF16)

    def frontend(src_rows, ntok, dstT, dst_key, c0, load_q="sync", keep_x=None):
        i = fe_cnt["i"]
        fe_cnt["i"] += 1
        par = i % 2
        x = keep_x if keep_x is not None else xb[par]
        xk = ("xk", id(keep_x)) if keep_x is not None else ("xb", par)
        h16 = h16b[par]
        sc = stat[:, 2 * par:2 * par + 2]
        sk = ("stat", par)
        P.ld(load_q, x[0:ntok, :], src_rows, reads=["x1_scr"] if keep_x is not None else [], writes=[xk])
        P.memset("gpsimd", sc[0:ntok, 0:1], 0.0, writes=[sk])
        P.act(junk[0][0:ntok, :], x[0:ntok, :], AF.Square, reads=[xk, sk], writes=["junk", sk], scale=1.0 / 32.0, accum_out=sc[0:ntok, 0:1])
        if DBG_FE < 2:
            return sc
        P.act(sc[0:ntok, 1:2], sc[0:ntok, 0:1], AF.Sqrt, reads=[sk, "epscol"], writes=[sk], bias=epscol[0:ntok, 0:1])
        P.recip(sc[0:ntok, 1:2], sc[0:ntok, 1:2], reads=[sk], writes=[sk])
        P.ts("vector", h16[0:ntok, :], x[0:ntok, :], sc[0:ntok, 1:2], None, ALU.mult, reads=[xk, sk], writes=[("h16", par)])
        if DBG_FE < 3:
            return sc
        for kt in range(8):
            P.tr(ptr_t[:, kt * 128:kt * 128 + ntok], h16[0:ntok, kt * 128:(kt + 1) * 128], ident16[0:ntok, 0:ntok],
                 reads=[("h16", par), "ident16"], writes=["ptr"])
        if DBG_FE < 4:
            return sc
        P.cp("scalar", dstT[:, :, c0:c0 + ntok], ptr_t[:, :].rearrange("p (a b) -> p a b", a=8)[:, :, 0:ntok], reads=["ptr"], writes=[dst_key])
        return sc

    wst = [None, None]
    w16 = [None, None]
    wt_cnt = {"i": 0}

    def wtile_alloc():
        for i in range(2):
            wst[i] = A.alloc([8, 128], F32)
            w16[i] = A.alloc([8, 128], BF16)

    def load_wtile(idx):
        i = wt_cnt["i"]
        wt_cnt["i"] += 1
        par = i % 2
        P.ld("sync", wst[par], w_tiles[idx, :, :, :], writes=[("wst", par)])
        P.tt("vector", w16[par], wst[par], gm.rearrange("p (a o) -> p a o", o=1).to_broadcast([128, 8, 128]), ALU.mult,
             reads=[("wst", par), "gm"], writes=[("w16", par)])
        return w16[par], ("w16", par)

    def proj(wt, wkey, srcT, skey, c0, n, bank, bkey):
        for kt in range(8):
            P.mm(bank[:, 0:n], wt[:, kt, :], srcT[:, kt, c0:c0 + n], start=(kt == 0), stop=(kt == 7), reads=[wkey, skey], writes=[bkey])

    def chunks(c0, c1, step=512):
        out = []
        c = c0
        while c < c1:
            n = min(step, c1 - c)
            out.append((c, n))
            c += n
        return out

    if ("A" in phases) or ("S" in phases):
        mwc = A.mark()
        wcs = A.alloc([16, 128], F32)
        for part in range(2):
            P.ld("sync", wcs, (wc_re_d if part == 0 else wc_im_d)[:, :, :], reads=[], writes=["wcs"])
            if part == 0:
                P.cp("vector", WC[:, 0, :, :], wcs, reads=["wcs"], writes=["WC"])
            else:
                P.ts("vector", WC[:, 1, :, :], wcs, -1.0, None, ALU.mult, reads=["wcs"], writes=["WC"])
        P.barrier()
        A.release(mwc)
        m0 = A.mark()
        sm = A.alloc([16, 3], F32)
        bre = A.alloc([16, 16], F32)
        bim = A.alloc([16, 16], F32)
        bbr = A.alloc([16, 16], F32)
        bbi = A.alloc([16, 16], F32)
        t3a = A.alloc([16, 16], F32)
        t3b = A.alloc([16, 16], F32)
        lps = A.alloc([16, 8], F32)
        lpc = A.alloc([16, 8], F32)
        lpm = A.alloc([16, 8], F32)
        lpr = A.alloc([16, 8], F32)
        lpi = A.alloc([16, 8], F32)
        e8t = A.alloc([4, 16, 4], F32)
        ZP = A.alloc([16, 128], F32)
        W8 = A.alloc([2, 8, 16 * 128], BF16) if "A" in phases else None
        P.ld("sync", sm, ssm_small[:, :, :], writes=["sm"])
        P.ld("sync", bre, b_re_d[:, :, :], writes=["bre"])
        P.ld("sync", bim, b_im_d[:, :, :], writes=["bim"])
        dtc = tmpc[:, 0, :]
        P.act(dtc, sm[:, :, 2], AF.Exp, reads=["sm"], writes=["dtc"])
        P.tt("vector", are, sm[:, :, 0], dtc, ALU.mult, reads=["sm", "dtc"], writes=["are"])
        P.tt("vector", th, sm[:, :, 1], dtc, ALU.mult, reads=["sm", "dtc"], writes=["th"])
        P.act(r1, are, AF.Exp, reads=["are"], writes=["r1"])
        for k in range(8):
            P.act(lpm[:, :, k], are, AF.Exp, reads=["are"], writes=["lpm"], scale=float(k))
        TK = ["tmpc"]
        ph, ph2, pp, ta, tb, tcc = (tmpc[:, i, :] for i in range(1, 7))
        P.ts("vector", ph, th, 1.0 / 16.0, None, ALU.mult, reads=["th"] + TK, writes=TK)
        P.tt("vector", ph2, ph, ph, ALU.mult, reads=TK, writes=TK)
        cur_re, cur_im = PWr[:, 0, :], PWi[:, 0, :]
        sc_ = [1.0, -1.0 / 6, 1.0 / 120, -1.0 / 5040, 1.0 / 362880, -1.0 / 39916800, 1.0 / 6227020800.0]
        cc_ = [1.0, -0.5, 1.0 / 24, -1.0 / 720, 1.0 / 40320, -1.0 / 3628800, 1.0 / 479001600, -1.0 / 87178291200.0]
        for coef, dst, mulphi in ((sc_, cur_im, True), (cc_, cur_re, False)):
            P.memset("vector", pp, coef[-1], writes=TK)
            for cf in coef[-2::-1]:
                P.tt("vector", pp, pp, ph2, ALU.mult, reads=TK, writes=TK)
                P.ts("vector", pp, pp, cf, None, ALU.add, reads=TK, writes=TK)
            if mulphi:
                P.tt("vector", dst, pp, ph, ALU.mult, reads=TK, writes=["PW"])
            else:
                P.cp("vector", dst, pp, reads=TK, writes=["PW"])

        def csq_norm(o_re, o_im, a_re, a_im):
            P.tt("vector", ta, a_re, a_re, ALU.mult, reads=TK + ["PW"], writes=TK)
            P.tt("vector", tb, a_im, a_im, ALU.mult, reads=TK + ["PW"], writes=TK)
            P.tt("vector", tcc, a_re, a_im, ALU.mult, reads=TK + ["PW"], writes=TK)
            P.tt("vector", o_re, ta, tb, ALU.subtract, reads=TK, writes=["PW"])
            P.ts("vector", o_im, tcc, 2.0, None, ALU.mult, reads=TK, writes=["PW"])
            P.tt("vector", ta, o_re, o_re, ALU.mult, reads=TK + ["PW"], writes=TK)
            P.tt("vector", tb, o_im, o_im, ALU.mult, reads=TK + ["PW"], writes=TK)
            P.tt("vector", ta, ta, tb, ALU.add, reads=TK, writes=TK)
            P.act(ta, ta, AF.Sqrt, reads=TK, writes=TK)
            P.recip(ta, ta, reads=TK, writes=TK)
            P.tt("vector", o_re, o_re, ta, ALU.mult, reads=TK + ["PW"], writes=["PW"])
            P.tt("vector", o_im, o_im, ta, ALU.mult, reads=TK + ["PW"], writes=["PW"])

        for _ in range(4):
            csq_norm(cur_re, cur_im, cur_re, cur_im)
        for k in range(1, NPW):
            csq_norm(PWr[:, k, :], PWi[:, k, :], PWr[:, k - 1, :], PWi[:, k - 1, :])

        def phasor_table(Er, Ei, T, k0, tmps, ekey):
            P.memset("vector", Er[:, :, 0:1], 1.0, writes=[ekey])
            P.memset("vector", Ei[:, :, 0:1], 0.0, writes=[ekey])
            n = 1
            k = k0
            while n < T:
                pr_b = PWr[:, k, :].rearrange("p (a o) -> p a o", o=1).to_broadcast([128, 16, n])
                pi_b = PWi[:, k, :].rearrange("p (a o) -> p a o", o=1).to_broadcast([128, 16, n])
                q = [t[:, :, 0:n] for t in tmps]
                P.tt("vector", q[0], Er[:, :, 0:n], pr_b, ALU.mult, reads=[ekey, "PW", "etmp"], writes=["etmp"])
                P.tt("vector", q[1], Ei[:, :, 0:n], pi_b, ALU.mult, reads=[ekey, "PW", "etmp"], writes=["etmp"])
                P.tt("vector", q[2], Er[:, :, 0:n], pi_b, ALU.mult, reads=[ekey, "PW", "etmp"], writes=["etmp"])
                P.tt("vector", q[3], Ei[:, :, 0:n], pr_b, ALU.mult, reads=[ekey, "PW", "etmp"], writes=["etmp"])
                P.tt("vector", Er[:, :, n:2 * n], q[0], q[1], ALU.subtract, reads=["etmp"], writes=[ekey])
                P.tt("vector", Ei[:, :, n:2 * n], q[2], q[3], ALU.add, reads=["etmp"], writes=[ekey])
                n *= 2
                k += 1

        phasor_table(lpc, lps, 8, 0, [e8t[:, i, :, :] for i in range(4)], "lpcs")
        fl = lambda t: t.rearrange("p a b -> p (a b)")
        P.tt("vector", fl(lpr), fl(lpm), fl(lpc), ALU.mult, reads=["lpm", "lpcs"], writes=["lpr"])
        P.tt("vector", fl(lpi), fl(lpm), fl(lps), ALU.mult, reads=["lpm", "lpcs"], writes=["lpi"])
        P.cp("vector", lb_re, lpr[:, :, 1], reads=["lpr"], writes=["lb_re"])
        P.cp("vector", lb_im, lpi[:, :, 1], reads=["lpi"], writes=["lb_im"])
        nre, d2, gre, gim, ta, tb = (tmpc[:, i, :] for i in range(1, 7))
        KT_ = ["tmpc"]
        P.ts("vector", nre, lb_re, -1.0, None, ALU.add, reads=["lb_re"] + KT_, writes=KT_)
        P.tt("vector", d2, sm[:, :, 0], sm[:, :, 0], ALU.mult, reads=["sm"] + KT_, writes=KT_)
        P.tt("vector", ta, sm[:, :, 1], sm[:, :, 1], ALU.mult, reads=["sm"] + KT_, writes=KT_)
        P.tt("vector", d2, d2, ta, ALU.add, reads=KT_, writes=KT_)
        P.recip(d2, d2, reads=KT_, writes=KT_)
        P.tt("vector", gre, nre, sm[:, :, 0], ALU.mult, reads=["sm"] + KT_, writes=KT_)
        P.tt("vector", ta, lb_im, sm[:, :, 1], ALU.mult, reads=["sm", "lb_im"] + KT_, writes=KT_)
        P.tt("vector", gre, gre, ta, ALU.add, reads=KT_, writes=KT_)
        P.tt("vector", gre, gre, d2, ALU.mult, reads=KT_, writes=KT_)
        P.tt("vector", gim, lb_im, sm[:, :, 0], ALU.mult, reads=["sm", "lb_im"] + KT_, writes=KT_)
        P.tt("vector", ta, nre, sm[:, :, 1], ALU.mult, reads=["sm"] + KT_, writes=KT_)
        P.tt("vector", gim, gim, ta, ALU.subtract, reads=KT_, writes=KT_)
        P.tt("vector", gim, gim, d2, ALU.mult, reads=KT_, writes=KT_)
        bc = lambda t: t.rearrange("p (a o) -> p a o", o=1).to_broadcast([128, 16, 16])
        P.tt("vector", t3a, bre, bc(gre), ALU.mult, reads=["bre"] + KT_, writes=["t3a"])
        P.tt("vector", t3b, bim, bc(gim), ALU.mult, reads=["bim"] + KT_, writes=["t3b"])
        P.tt("vector", bbr, t3a, t3b, ALU.subtract, reads=["t3a", "t3b"], writes=["bbr"])
        P.tt("vector", t3a, bim, bc(gre), ALU.mult, reads=["bim", "bbr"] + KT_, writes=["t3a"])
        P.tt("vector", t3b, bre, bc(gim), ALU.mult, reads=["bre", "bbr"] + KT_, writes=["t3b"])
        P.tt("vector", bbi, t3a, t3b, ALU.add, reads=["t3a", "t3b"], writes=["bbi"])
        P.memset("gpsimd", ZP, 0.0, writes=["ZP"])
        svals = (range(8) if "A" in phases else [7]) if not (DBG_SKIP & 16) else [7]
        for s in svals:
            k = 7 - s
            fr = lambda t: t[:, :, k:k + 1].to_broadcast([128, 16, 16])
            for part in range(2):
                if part == 0:
                    P.tt("vector", t3a, bbr, fr(lpr), ALU.mult, reads=["bbr", "lpr", "ZP"], writes=["t3a"])
                    P.tt("vector", t3b, bbi, fr(lpi), ALU.mult, reads=["bbi", "lpi", "ZP"], writes=["t3b"])
                    P.tt("vector", t3a, t3a, t3b, ALU.subtract, reads=["t3a", "t3b"], writes=["t3a"])
                else:
                    P.tt("vector", t3a, bbi, fr(lpr), ALU.mult, reads=["bbi", "lpr", "ZP"], writes=["t3a"])
                    P.tt("vector", t3b, bbr, fr(lpi), ALU.mult, reads=["bbr", "lpi", "ZP"], writes=["t3b"])
                    P.tt("vector", t3a, t3a, t3b, ALU.add, reads=["t3a", "t3b"], writes=["t3a"])
                for pm in range(4):
                    for gl in range(2):
                        P.cp("vector", ZP[64 * gl:64 * gl + 64, pm::4, 32 * pm + 16 * gl:32 * pm + 16 * gl + 16],
                             t3a[64 * gl:64 * gl + 64, pm::4, :], reads=["t3a"], writes=["ZP"])
                for q4 in range(4):
                    bank, bkey = nextbank()
                    for j in range(4):
                        pr = q4 * 4 + j
                        P.tr(bank[:, j * 128:(j + 1) * 128], ZP[:, pr, :], ident32, reads=["ZP", "ident32"], writes=[bkey])
                    if "A" in phases:
                        P.cp("vector", W8[:, part, s, q4 * 512:(q4 + 1) * 512], bank, reads=[bkey], writes=["W8"])
                    if s == 7:
                        P.cp("vector", WB[:, part, q4 * 4:(q4 + 1) * 4, :], bank.rearrange("p (a b) -> p a b", a=4), reads=[bkey], writes=["WB"])
        P.memset("gpsimd", gi_re, 0.0, writes=["gi"])
        P.memset("gpsimd", gi_im, 0.0, writes=["gi"])

    if "A" in phases and not (DBG_SKIP & 8):
        J = 128
        r8 = A.alloc([16], F32)
        C8 = A.alloc([16, J], F32)
        S8 = A.alloc([16, J], F32)
        e8tmp = [A.alloc([16, J // 2], F32) for _ in range(4)]
        P.act(r8, are, AF.Exp, reads=["are"], writes=["r8"], scale=8.0)
        if not (DBG_SKIP & 1):
            phasor_table(C8, S8, J, 3, e8tmp, "CS8")
        cJ = PWr[:, 10, :]
        sJ = PWi[:, 10, :]
        frontend_alloc()
        wust = A.alloc([8, 128], F32)
        wu16 = A.alloc([4, 8, 128], BF16)
        for ct in range(0 if (DBG_SKIP & 2) else 4):
            P.ld("sync", wust, w_tiles[TI[("u", ct)], :, :, :], writes=["wust"])
            P.tt("vector", wu16[:, ct, :, :], wust, gm.rearrange("p (a o) -> p a o", o=1).to_broadcast([128, 8, 128]), ALU.mult, reads=["wust", "gm"], writes=["wu16"])
        CH = 1024
        hTc = [A.alloc([8, CH], BF16) for _ in range(2)]
        uTc = [A.alloc([4, CH], BF16)] * 2
        Rre = A.alloc([J], F32)
        Rim = A.alloc([J], F32)
        Gre = A.alloc([J], F32)
        Gim = A.alloc([J], F32)
        q1 = A.alloc([J], F32)
        q2 = A.alloc([J], F32)
        nch = min(TPRI // CH, DBG_NCH)
        for ch in range(nch):
            par = ch % 2
            hk = ("hTc", par)
            uk = "uTc"
            for t in range(min(CH // 128, DBG_NT)):
                r0 = ch * CH + t * 128
                frontend(x_all[r0:r0 + 128, :], 128, hTc[par], hk, t * 128)
            for ct in range(4 if DBG_STAGE >= 2 else 0):
                for (c0, n) in chunks(0, CH):
                    bank, bkey = nextbank(banks)
                    proj(wu16[:, ct, :, :], "wu16", hTc[par], hk, c0, n, bank, bkey)
                    P.cp("scalar", uTc[par][:, ct, c0:c0 + n], bank[:, 0:n], reads=[bkey], writes=[uk])
            for pr in range(16 if DBG_STAGE >= 3 else 0):
                ct = pr // 4
                bank, bkey = nextbank(banks)
                xre = bank[:, 0:J]
                xim = bank[:, J:2 * J]
                for part, dst in ((0, xre), (1, xim)):
                    for s in range(8):
                        P.mm(dst, W8[:, part, s, pr * 128:(pr + 1) * 128], uTc[par][:, ct, s:CH:8], start=(s == 0), stop=(s == 7),
                             reads=["W8", uk], writes=[bkey])
                if DBG_STAGE < 4:
                    continue
                c8 = C8[:, pr, :]
                s8 = S8[:, pr, :]
                P.tt("vector", q1, xre, c8, ALU.mult, reads=[bkey, "CS8"], writes=["q1"])
                P.tt("vector", q2, xim, s8, ALU.mult, reads=[bkey, "CS8"], writes=["q2"])
                P.tt("vector", Rre, q1, q2, ALU.add, reads=["q1", "q2"], writes=["Rre"])
                P.tt("vector", q1, xim, c8, ALU.mult, reads=[bkey, "CS8", "Rre"], writes=["q1"])
                P.tt("vector", q2, xre, s8, ALU.mult, reads=[bkey, "CS8", "Rre"], writes=["q2"])
                P.tt("vector", Rim, q1, q2, ALU.subtract, reads=["q1", "q2"], writes=["Rim"])
                P.scan(Gre, r8[:, pr:pr + 1].to_broadcast([128, J]), Rre, gi_re[:, pr:pr + 1], reads=["Rre", "r8", "gi"], writes=["Gre"])
                P.scan(Gim, r8[:, pr:pr + 1].to_broadcast([128, J]), Rim, gi_im[:, pr:pr + 1], reads=["Rim", "r8", "gi"], writes=["Gim"])
                ta_ = tmpc[:, 0, pr:pr + 1]
                tb_ = tmpc[:, 1, pr:pr + 1]
                P.ts("vector", ta_, Gim[:, J - 1:J], sJ[:, pr:pr + 1], None, ALU.mult, reads=["Gim", "PW"], writes=["tmpc"])
                P.ts("vector", tb_, Gre[:, J - 1:J], sJ[:, pr:pr + 1], None, ALU.mult, reads=["Gre", "PW"], writes=["tmpc"])
                P.stt("vector", gi_re[:, pr:pr + 1], Gre[:, J - 1:J], cJ[:, pr:pr + 1], ta_, ALU.mult, ALU.subtract, reads=["Gre", "PW", "tmpc"], writes=["gi"])
                P.stt("vector", gi_im[:, pr:pr + 1], Gim[:, J - 1:J], cJ[:, pr:pr + 1], tb_, ALU.mult, ALU.add, reads=["Gim", "PW", "tmpc"], writes=["gi"])
        if DBG_SKIP & 4:
            phases = phases.replace("A", "a")
        c7 = lpc[:, :, 7]
        s7 = lps[:, :, 7]
        ta, tb = tmpc[:, 2, :], tmpc[:, 3, :]
        P.tt("vector", ta, gi_im, s7, ALU.mult, reads=["gi", "lpcs"], writes=["tmpc"])
        P.tt("vector", tb, gi_re, s7, ALU.mult, reads=["gi", "lpcs"], writes=["tmpc"])
        P.tt("vector", gi_re, gi_re, c7, ALU.mult, reads=["gi", "lpcs"], writes=["gi"])
        P.tt("vector", gi_re, gi_re, ta, ALU.add, reads=["gi", "tmpc"], writes=["gi"])
        P.tt("vector", gi_im, gi_im, c7, ALU.mult, reads=["gi", "lpcs"], writes=["gi"])
        P.tt("vector", gi_im, gi_im, tb, ALU.subtract, reads=["gi", "tmpc"], writes=["gi"])
    P.barrier()
    A.release(base_mark)
    if dbg == "A":
        dd = A.alloc([4096], F32)
        P.memset("gpsimd", dd, 0.0, writes=["dd"])
        P.cp("vector", dd[:, 0:16], gi_re, reads=["gi"], writes=["dd"])
        P.cp("vector", dd[:, 16:32], gi_im, reads=["gi"], writes=["dd"])
        P.cp("vector", dd[:, 32:48], th, reads=["th"], writes=["dd"])
        P.cp("vector", dd[:, 48:64], r1, reads=["r1"], writes=["dd"])
        P.cp("vector", dd[:, 64:80], lb_re, reads=["lb_re"], writes=["dd"])
        P.cp("vector", dd[:, 80:96], lb_im, reads=["lb_im"], writes=["dd"])
        P.cp("vector", dd[:, 128:128 + 2048], WB[:, 0, :, :].rearrange("p a b -> p (a b)"), reads=["WB"], writes=["dd"])
        P.ld("sync", dbg_o[:, :], dd, reads=["dd"])


    attnT = A.alloc([4, TOWN + NDEC], BF16)
    pers_mark = A.mark()
    hTo = A.alloc([8, TOWN + NDEC], BF16)
    own_mark = A.mark()
    hTh = A.alloc([8, TOWN], BF16)
    hT_mark = A.mark()

    class _HT:
        def __getitem__(self, idx):
            p, kt, cs = idx
            st = cs.start
            sd = cs.step or 1
            cnt = (cs.stop - st + sd - 1) // sd
            stop = st + (cnt - 1) * sd + 1
            if st < TOWN:
                assert stop <= TOWN
                return hTh[p, kt, slice(st, stop, sd)]
            return hTo[p, kt, slice(st - TOWN, stop - TOWN, sd)]

    hT = _HT()

    if "B" in phases:
        frontend_alloc()
        for t in range(32):
            r0 = TPRI - TOWN + t * 128
            if t < 16:
                frontend(x_all[r0:r0 + 128, :], 128, hTh, "hT", t * 128)
            else:
                frontend(x_all[r0:r0 + 128, :], 128, hTo, "hT", (t - 16) * 128)
        frontend(x_dec[:, :], NDEC, hTo, "hT", TOWN)
        P.barrier()
        A.release(hT_mark)
        ropeC = A.alloc([TALL], BF16)
        ropeS = A.alloc([TALL], BF16)
        m_r = A.mark()
        rtmp = A.alloc([1028], F32)
        for (tab, src, key) in ((ropeC, rope_c_d, "ropeC"), (ropeS, rope_s_d, "ropeS")):
            for (c0, n) in chunks(0, TALL, 1028):
                P.ld("sync", rtmp[:, 0:n], src[:, c0:c0 + n], writes=["rtmp"])
                P.cp("vector", tab[:, c0:c0 + n], rtmp[:, 0:n], reads=["rtmp"], writes=[key])
        P.barrier()
        A.release(m_r)
        mask2 = A.alloc([256], BF16)
        maskh = A.alloc([128], BF16)
        mtmp = A.alloc([256], F32)
        P.ld("sync", mtmp, mask2_d[:, :], writes=["mtmp"])
        P.cp("vector", mask2, mtmp, reads=["mtmp"], writes=["mask2"])
        P.ld("sync", mtmp[:, 0:128], maskh_d[:, :], reads=["mask2"], writes=["mtmp"])
        P.cp("vector", maskh, mtmp[:, 0:128], reads=["mtmp"], writes=["maskh"])
        QTd = A.alloc([12, NDEC], BF16)
        KTd = A.alloc([12, NDEC], BF16)
        VTd = A.alloc([12, NDEC], F32)
        m_dec = A.mark()
        wtile_alloc()
        QT = [A.alloc([TOWN + NDEC], BF16) for g in range(3)]
        KT = [A.alloc([TALL - KSTART[g]], BF16) for g in range(3)]
        NBLK = (17, 20, 32)
        VG = [A.alloc([NBLK[g], 192], BF16) for g in range(3)]
        for g in range(3):
            P.memset("gpsimd", VG[g][:, :, 64:128], 1.0, writes=[("VG", g)])
        t1b = [A.alloc([512], F32) for _ in range(2)]
        t2b = [A.alloc([512], F32) for _ in range(2)]
        PTb = [A.alloc([256], BF16) for _ in range(3)]
        Rb = [A.alloc([512], F32) for _ in range(2)]
        ev = {"i": 0}

        def rope_evac(bA, kA, bB, kB, c0, n, dst, dkey):
            i = ev["i"] % 2
            ev["i"] += 1
            P.tt("vector", t1b[i][:, 0:n], bA[:, 0:n], ropeC[:, c0:c0 + n], ALU.mult, reads=[kA, "ropeC"], writes=[("t1", i)])
            P.tt("vector", t2b[i][:, 0:n], bB[:, 0:n], ropeS[:, c0:c0 + n], ALU.mult, reads=[kB, "ropeS"], writes=[("t2", i)])
            P.tt("vector", dst, t1b[i][:, 0:n], t2b[i][:, 0:n], ALU.add, reads=[("t1", i), ("t2", i)], writes=[dkey])

        def vblocks(g):
            out = []
            if g == 0:
                for b in range(17):
                    out.append((1920 + 128 * b, 1))
            elif g == 1:
                for r in range(4):
                    for kbi in range(5):
                        out.append((1536 + 512 * kbi + r, 4))
            else:
                for r in range(16):
                    for kbi in range(2):
                        out.append((2048 * kbi + r, 16))
            return out

        pt_cnt = {"i": 0}
        for hp in range(4 if DBG_STAGE >= 9 else 1):
            for g in range(3):
                wm, kwm = load_wtile(TI[("q", g, hp)])
                wp, kwp = load_wtile(TI[("qp", g, hp)])
                for (c0, n) in chunks(TOWN, TALL):
                    bA, kA = nextbank()
                    proj(wm, kwm, hT, "hT", c0, n, bA, kA)
                    bB, kB = nextbank()
                    proj(wp, kwp, hT, "hT", c0, n, bB, kB)
                    rope_evac(bA, kA, bB, kB, c0, n, QT[g][:, c0 - TOWN:c0 - TOWN + n], ("QT", g))
                P.cp("vector", QTd[:, g * 4 + hp, :], QT[g][:, TOWN:TOWN + NDEC], reads=[("QT", g)], writes=["QTd"])
                wm, kwm = load_wtile(TI[("k", g, hp)])
                wp, kwp = load_wtile(TI[("kp", g, hp)])
                ks = KSTART[g]
                for (c0, n) in chunks(ks, TOWN) + chunks(TOWN, TALL):
                    bA, kA = nextbank()
                    proj(wm, kwm, hT, "hT", c0, n, bA, kA)
                    bB, kB = nextbank()
                    proj(wp, kwp, hT, "hT", c0, n, bB, kB)
                    rope_evac(bA, kA, bB, kB, c0, n, KT[g][:, c0 - ks:c0 - ks + n], ("KT", g))
                P.cp("vector", KTd[:, g * 4 + hp, :], KT[g][:, TEXT - ks:TEXT - ks + NDEC], reads=[("KT", g)], writes=["KTd"])
                wv, kwv = load_wtile(TI[("v", g, hp)])
                blks = vblocks(g)
                for b0 in range(0, len(blks), 4):
                    nb = min(4, len(blks) - b0)
                    bank, bkey = nextbank()
                    for j in range(nb):
                        st, sd = blks[b0 + j]
                        for kt in range(8):
                            P.mm(bank[:, j * 128:(j + 1) * 128], hT[:, kt, st:st + 128 * sd:sd], wv[:, kt, :], start=(kt == 0), stop=(kt == 7),
                                 reads=["hT", kwv], writes=[bkey])
                    bv = bank.rearrange("p (a b) -> p a b", a=4)
                    P.cp("scalar", VG[g][:, b0:b0 + nb, 0:64], bv[:, 0:nb, 0:64], reads=[bkey], writes=[("VG", g)])
                    P.cp("scalar", VG[g][:, b0:b0 + nb, 128:192], bv[:, 0:nb, 64:128], reads=[bkey], writes=[("VG", g)])
                bank, bkey = nextbank()
                proj(wv, kwv, hT, "hT", TEXT, NDEC, bank, bkey)
                P.cp("scalar", VTd[:, g * 4 + hp, :], bank[:, 0:NDEC], reads=[bkey], writes=["VTd"])
            for hl in range(2):
                hr = slice(64 * hl, 64 * hl + 64)
                nrows = slice(64 * hl, 64 * hl + 64)
                lrows = slice(64 - 64 * hl, 128 - 64 * hl)
                vsel = slice(0, 128) if hl == 0 else slice(64, 192)
                for k4 in range(4):
                    P.mm(pacc[:, 512 * k4:512 * (k4 + 1)], zeros16, hTo[:, 0, 0:512], start=True, stop=False,
                         reads=["zeros16", "hT"], writes=["pacc%d" % k4])
                for g in range(3):
                    ks = KSTART[g]
                    if g == 0:
                        units = [(0, kb) for kb in range(-1, 16)]
                        nq = 16
                    elif g == 1:
                        units = [(r, kb) for r in range(4) for kb in range(-1, 4)]
                        nq = 4
                    else:
                        units = [(r, kb) for r in range(16) for kb in range(-1, 1)]
                        nq = 1
                    for (r, kb) in units:
                        if g == 0:
                            blk = kb + 1
                            kst, ksd = 2048 + 128 * kb, 1
                        elif g == 1:
                            blk = r * 5 + kb + 1
                            kst, ksd = 2048 + 512 * kb + r, 4
                        else:
                            blk = r * 2 + kb + 1
                            kst, ksd = 2048 + 2048 * kb + r, 16
                        kcols = KT[g][hr, kst - ks:kst - ks + 128 * ksd:ksd]
                        roles = []
                        if kb >= 0:
                            roles.append((kb, mask2[:, 0:128], "mask2"))
                        if kb + 1 < nq:
                            roles.append((kb + 1, (maskh if kb == -1 else mask2[:, 128:256]), ("maskh" if kb == -1 else "mask2")))
                        sb, skey = nextbank()
                        half = (pt_cnt["i"] // 3) % 2
                        pti = pt_cnt["i"] % 3
                        pt_cnt["i"] += 1
                        S = sb[:, 0:256]
                        for ri, (qb, mk, mkey) in enumerate(roles):
                            if g == 0:
                                qst, qsd = 128 * qb, 1
                            elif g == 1:
                                qst, qsd = 512 * qb + r, 4
                            else:
                                qst, qsd = r, 16
                            qcols = QT[g][hr, qst:qst + 128 * qsd:qsd]
                            P.mm(S[:, ri * 128:(ri + 1) * 128], kcols, qcols, start=True, stop=False, reads=[("KT", g), ("QT", g)], writes=[skey])
                            P.mm(S[:, ri * 128:(ri + 1) * 128], ident16, mk, start=False, stop=True, reads=["ident16", mkey], writes=[skey])
                        nr = len(roles)
                        PT = PTb[pti]
                        P.act(PT[:, 0:nr * 128], S[:, 0:nr * 128], AF.Exp, reads=[skey], writes=[("PT", pti)], scale=0.125)
                        for ri, (qb, mk, mkey) in enumerate(roles):
                            lhs = VG[g][:, blk, vsel]
                            if g == 0:
                                P.mm(pacc[:, 128 * qb:128 * qb + 128], lhs, PT[:, ri * 128:(ri + 1) * 128], start=False, stop=False,
                                     reads=[("VG", g), ("PT", pti)], writes=["pacc%d" % (qb // 4)])
                            elif g == 1:
                                P.mm(pacc[:, 512 * qb + r:512 * qb + 512:4], lhs, PT[:, ri * 128:(ri + 1) * 128], start=False, stop=False,
                                     reads=[("VG", g), ("PT", pti)], writes=["pacc%d" % qb])
                            else:
                                for k4 in range(4):
                                    P.mm(pacc[:, 512 * k4 + r:512 * k4 + 512:16], lhs, PT[:, ri * 128 + 32 * k4:ri * 128 + 32 * k4 + 32], start=False, stop=False,
                                         reads=[("VG", g), ("PT", pti)], writes=["pacc%d" % k4])
                for k4 in range(4):
                    Rk = Rb[k4 % 2]
                    P.recip(Rk[lrows, :], pacc[lrows, 512 * k4:512 * (k4 + 1)], reads=["pacc%d" % k4], writes=[("Rb", k4 % 2)])
                    P.tt("vector", attnT[nrows, hp, 512 * k4:512 * (k4 + 1)], pacc[nrows, 512 * k4:512 * (k4 + 1)], Rk[lrows, :], ALU.mult,
                         reads=["pacc%d" % k4, ("Rb", k4 % 2)], writes=["attnT"])

        P.barrier()
        A.release(m_dec)
        if "F" in phases:
            m_f = A.mark()
            kcb = [A.alloc([2, 512], F32) for _ in range(2)]
            prod = A.alloc([512], F32)
            s8 = A.alloc([8], F32)
            p8 = A.alloc([8], F32)
            om = A.alloc([512], F32)
            lcs = A.alloc([1], F32)
            EM = A.alloc([512], F32)
            bones = A.alloc([128], F32)
            pq = A.alloc([NDEC], F32)
            psf = A.alloc([NDEC], F32)
            NDs = A.alloc([4, NDEC], F32)
            LDs = A.alloc([4, NDEC], F32)
            tq = A.alloc([NDEC], F32)
            P.ld("sync", EM[0:8, :], em_d[:, :], writes=["EM"])
            P.ld("sync", bones, bones_d[:, :], writes=["bones"])
            P.memset("gpsimd", NDs, 0.0, writes=["NDs"])
            P.memset("gpsimd", LDs, 0.0, writes=["LDs"])
            ndld = pacc[:, 1536:1664]
            NDp = ndld[:, 0:64].rearrange("p (a b) -> p a b", a=4)
            LDp = ndld[:, 64:128].rearrange("p (a b) -> p a b", a=4)
            P.mm(ndld, zeros16, hTo[:, 0, 0:128], start=True, stop=False, reads=["zeros16", "hT"], writes=["pacc3"])
            ci = 0
            for b in range(NDEC):
                for g in range(3):
                    par = ci % 2
                    ci += 1
                    kc = kcb[par]
                    P.ld("sync", kc, caches[g][b, 0:LC[g]:DIL[g], :, :], writes=[("kc", par)])
                    qb_, qkey = nextbank()
                    for hp in range(4):
                        P.mm(qb_[:, hp * 128:(hp + 1) * 128], QTd[:, g * 4 + hp, b:b + 1].to_broadcast([128, 128]), ident16, reads=["QTd", "ident16"], writes=[qkey])
                    P.tt("vector", prod, qb_, kc[:, 0, :], ALU.mult, reads=[qkey, ("kc", par)], writes=["prod"])
                    P.red("vector", s8, prod.rearrange("p (h e) -> p h e", e=64), ALU.add, AX.X, reads=["prod"], writes=["s8"])
                    P.act(p8, s8, AF.Exp, reads=["s8"], writes=["p8"], scale=0.125)
                    ob, okey = nextbank()
                    P.mm(ob[0:8, 0:512], p8[:, 0:8], kc[:, 1, :], reads=["p8", ("kc", par)], writes=[okey])
                    lb_, lkey = nextbank()
                    P.mm(lb_[0:8, 0:1], p8[:, 0:8], ones32[:, 0:1], reads=["p8", "ones32"], writes=[lkey])
                    P.tt("vector", om[0:8, :], ob[0:8, 0:512], EM[0:8, :], ALU.mult, reads=[okey, "EM"], writes=["om"])
                    P.cp("vector", lcs[0:8, :], lb_[0:8, 0:1], reads=[lkey], writes=["lcs"])
                    for hp in range(4):
                        P.mm(NDp[:, hp, b:b + 1], om[0:8, hp * 128:(hp + 1) * 128], ones32[0:8, 0:1], start=False, stop=False, reads=["om", "ones32"], writes=["pacc3"])
                        P.mm(LDp[:, hp, b:b + 1], EM[0:8, hp * 128:(hp + 1) * 128], lcs[0:8, 0:1], start=False, stop=False, reads=["EM", "lcs"], writes=["pacc3"])
            for g in range(3):
                for hp in range(4):
                    idx = g * 4 + hp
                    P.tt("vector", pq, QTd[:, idx, :], KTd[:, idx, :], ALU.mult, reads=["QTd", "KTd"], writes=["pq"])
                    sb_, skey = nextbank()
                    P.mm(sb_[:, 0:NDEC], bones, pq, reads=["bones", "pq"], writes=[skey])
                    P.act(psf, sb_[:, 0:NDEC], AF.Exp, reads=[skey], writes=["psf"], scale=0.125)
                    P.tt("vector", tq, psf, VTd[:, idx, :], ALU.mult, reads=["psf", "VTd"], writes=["tq"])
                    P.tt("vector", NDs[:, hp, :], NDs[:, hp, :], tq, ALU.add, reads=["NDs", "tq"], writes=["NDs"])
                    P.tt("vector", LDs[:, hp, :], LDs[:, hp, :], psf, ALU.add, reads=["LDs", "psf"], writes=["LDs"])
            P.tt("vector", NDs, NDs, NDp, ALU.add, reads=["NDs", "pacc3"], writes=["NDs"])
            P.tt("vector", LDs, LDs, LDp, ALU.add, reads=["LDs", "pacc3"], writes=["LDs"])
            P.recip(LDs, LDs, reads=["LDs"], writes=["LDs"])
            P.tt("vector", attnT[:, :, TOWN:TOWN + NDEC], NDs, LDs, ALU.mult, reads=["NDs", "LDs"], writes=["attnT"])
            P.barrier()
            A.release(m_f)
        if "E" in phases:
            m_e = A.mark()
            rtm = A.alloc([17, 16], F32)
            wkst = A.alloc([8, 512], F32)
            wk16 = A.alloc([8, 512], BF16)
            o32 = [A.alloc([8, 64], F32) for _ in range(2)]
            rt = [A.alloc([8, 8], F32) for _ in range(4)]
            P.ld("sync", rtm, rope_tm_d[:, :, :], writes=["rtm"])
            oi = 0
            for kv in range(2):
                for g in range(3):
                    P.ld("sync", wkst, w_kv[kv * 3 + g, :, :, :], writes=["wkst"])
                    P.tt("vector", wk16, wkst, gm.rearrange("p (a o) -> p a o", o=1).to_broadcast([128, 8, 512]), ALU.mult, reads=["wkst", "gm"], writes=["wk16"])
                    tl = {0: [15], 1: [12, 13, 14, 15], 2: list(range(16))}[g] + [16]
                    for t in tl:
                        ntok = 128 if t < 16 else NDEC
                        par = oi % 2
                        oi += 1
                        ov = o32[par]
                        okey2 = ("o32", par)
                        bank, bkey = nextbank()
                        for kt in range(8):
                            P.mm(bank[0:ntok, 0:512], hTo[:, kt, t * 128:t * 128 + ntok], wk16[:, kt, :], start=(kt == 0), stop=(kt == 7), reads=["hT", "wk16"], writes=[bkey])
                        P.cp("scalar", ov[0:ntok, :, :], bank[0:ntok, 0:512].rearrange("p (h e) -> p h e", e=64), reads=[bkey], writes=[okey2])
                        if kv == 0:
                            cb = rtm[0:ntok, t, 0:8].rearrange("p (o e) -> p o e", o=1).to_broadcast([ntok, 8, 8])
                            sbb = rtm[0:ntok, t, 8:16].rearrange("p (o e) -> p o e", o=1).to_broadcast([ntok, 8, 8])
                            x1v, x2v = ov[0:ntok, :, 0:8], ov[0:ntok, :, 8:16]
                            P.tt("vector", rt[0][0:ntok], x1v, cb, ALU.mult, reads=[okey2, "rtm"], writes=["rt0"])
                            P.tt("vector", rt[1][0:ntok], x2v, sbb, ALU.mult, reads=[okey2, "rtm"], writes=["rt1"])
                            P.tt("vector", rt[2][0:ntok], x2v, cb, ALU.mult, reads=[okey2, "rtm"], writes=["rt2"])
                            P.tt("vector", rt[3][0:ntok], x1v, sbb, ALU.mult, reads=[okey2, "rtm"], writes=["rt3"])
                            P.tt("vector", x1v, rt[0][0:ntok], rt[1][0:ntok], ALU.subtract, reads=["rt0", "rt1"], writes=[okey2])
                            P.tt("vector", x2v, rt[2][0:ntok], rt[3][0:ntok], ALU.add, reads=["rt2", "rt3"], writes=[okey2])
                        src = ov[0:ntok, :, :].rearrange("p h e -> p (h e)")
                        if t < 16:
                            r0 = (t - tl[0]) * 128
                            P.ld("sync", kvp[g][r0:r0 + 128, kv, :], src, reads=[okey2])
                        else:
                            P.ld("sync", kvs[g][:, LC[g] - 1, kv, :], src, reads=[okey2])
            P.barrier()
            A.release(m_e)
        if dbg == "B":
            P.barrier()
            A.release(m_dec)
            dd = A.alloc([4096], F32)
            P.memset("gpsimd", dd, 0.0, writes=["dd"])
            P.cp("vector", dd[:, 0:2048], attnT[:, 0, 0:2048], reads=["attnT"], writes=["dd"])
            P.cp("vector", dd[:, 2048:4096], hTo[:, 0, 0:2048], reads=["hT"], writes=["dd"])
            P.ld("sync", dbg_o[:, :], dd, reads=["dd"])
        P.barrier()
        A.release(own_mark)


    y2T = A.alloc([4, TOWN + NDEC], BF16)
    y2_mark = A.mark()
    if "S" in phases and "B" in phases:
        TC = 256
        uT = A.alloc([4, TOWN + NDEC], BF16)
        C1 = A.alloc([16, TC], F32)
        S1 = A.alloc([16, TC], F32)
        m_t = A.mark()
        etmp = [A.alloc([16, 64], F32) for _ in range(4)]
        phasor_table(C1[:, :, 0:128], S1[:, :, 0:128], 128, 0, etmp, "CS1")
        P.barrier()
        A.release(m_t)
        Rre = A.alloc([TC], F32)
        Rim = A.alloc([TC], F32)
        Gre = A.alloc([TC], F32)
        Gim = A.alloc([TC], F32)
        q1 = A.alloc([TC], F32)
        q2 = A.alloc([TC], F32)
        for pr in range(16):
            prc = PWr[:, 7, pr:pr + 1].to_broadcast([128, 128])
            pic = PWi[:, 7, pr:pr + 1].to_broadcast([128, 128])
            P.tt("vector", Rre[:, 0:128], C1[:, pr, 0:128], prc, ALU.mult, reads=["CS1", "PW", "Rre"], writes=["Rre"])
            P.tt("vector", Rim[:, 0:128], S1[:, pr, 0:128], pic, ALU.mult, reads=["CS1", "PW", "Rim"], writes=["Rim"])
            P.tt("vector", Gre[:, 0:128], C1[:, pr, 0:128], pic, ALU.mult, reads=["CS1", "PW", "Gre"], writes=["Gre"])
            P.tt("vector", Gim[:, 0:128], S1[:, pr, 0:128], prc, ALU.mult, reads=["CS1", "PW", "Gim"], writes=["Gim"])
            P.tt("vector", C1[:, pr, 128:256], Rre[:, 0:128], Rim[:, 0:128], ALU.subtract, reads=["Rre", "Rim"], writes=["CS1"])
            P.tt("vector", S1[:, pr, 128:256], Gre[:, 0:128], Gim[:, 0:128], ALU.add, reads=["Gre", "Gim"], writes=["CS1"])
        cT = PWr[:, 8, :]
        sT = PWi[:, 8, :]
        H16 = [A.alloc([TC], BF16) for _ in range(2)]
        wtile_alloc()
        wgst = A.alloc([4, 512], F32)
        wg16 = A.alloc([4, 512], BF16)
        dsk = A.alloc([4], F32)
        bgl = A.alloc([4], F32)
        h0t = A.alloc([16, NDEC, 2], F32)
        hsd = A.alloc([16, NDEC, 2], F32)
        hsp = A.alloc([16, 2], F32)
        yv = A.alloc([4, TC], F32)
        yg = A.alloc([4, TC], F32)
        yg16 = A.alloc([4, TC], BF16)
        sg = A.alloc([TC], F32)
        P.ld("sync", wgst, w_glu[:, :, :], writes=["wgst"])
        P.cp("vector", wg16, wgst, reads=["wgst"], writes=["wg16"])
        P.ld("sync", dsk, dskip_d[:, :], writes=["dsk"])
        P.ld("sync", bgl, bglu_d[:, :], writes=["bgl"])
        P.ld("sync", h0t, h0_d[:, :, :, :], writes=["h0t"])
        for ct in range(4):
            wu, kwu = load_wtile(TI[("u", ct)])
            for (c0, n) in chunks(TOWN, TALL):
                bank, bkey = nextbank()
                proj(wu, kwu, hT, "hT", c0, n, bank, bkey)
                P.cp("scalar", uT[:, ct, c0 - TOWN:c0 - TOWN + n], bank[:, 0:n], reads=[bkey], writes=["uT"])
        ybanks = banks[0:4]
        nchunk = TOWN // TC
        for kc in list(range(nchunk)) + ["dec"]:
            dec = kc == "dec"
            o0, n = (TOWN, NDEC) if dec else (kc * TC, TC)
            for ct in range(4):
                yb, ykey = ybanks[ct]
                for pm in range(4):
                    pr = ct * 4 + pm
                    bank, bkey = nextbank()
                    bur = bank[:, 0:n]
                    bui = bank[:, 256:256 + n]
                    P.mm(bur, WB[:, 0, pr, :], uT[:, ct, o0:o0 + n], reads=["WB", "uT"], writes=[bkey])
                    P.mm(bui, WB[:, 1, pr, :], uT[:, ct, o0:o0 + n], reads=["WB", "uT"], writes=[bkey])
                    hre16, him16 = H16[0][:, 0:n], H16[1][:, 0:n]
                    if not dec:
                        c1 = C1[:, pr, :]
                        s1 = S1[:, pr, :]
                        P.tt("vector", q1, bur, c1, ALU.mult, reads=[bkey, "CS1"], writes=["q1"])
                        P.tt("vector", q2, bui, s1, ALU.mult, reads=[bkey, "CS1"], writes=["q2"])
                        P.tt("vector", Rre, q1, q2, ALU.add, reads=["q1", "q2"], writes=["Rre"])
                        P.tt("vector", q1, bui, c1, ALU.mult, reads=[bkey, "CS1", "Rre"], writes=["q1"])
                        P.tt("vector", q2, bur, s1, ALU.mult, reads=[bkey, "CS1", "Rre"], writes=["q2"])
                        P.tt("vector", Rim, q1, q2, ALU.subtract, reads=["q1", "q2"], writes=["Rim"])
                        P.scan(Gre, r1[:, pr:pr + 1].to_broadcast([128, TC]), Rre, gi_re[:, pr:pr + 1], reads=["Rre", "r1", "gi"], writes=["Gre"])
                        P.scan(Gim, r1[:, pr:pr + 1].to_broadcast([128, TC]), Rim, gi_im[:, pr:pr + 1], reads=["Rim", "r1", "gi"], writes=["Gim"])
                        ta_ = tmpc[:, 0, pr:pr + 1]
                        tb_ = tmpc[:, 1, pr:pr + 1]
                        if kc == nchunk - 1:
                            cl, sl_ = C1[:, pr, TC - 1:TC], S1[:, pr, TC - 1:TC]
                            P.tt("vector", ta_, Gim[:, TC - 1:TC], sl_, ALU.mult, reads=["Gim", "CS1"], writes=["tmpc"])
                            P.tt("vector", tb_, Gre[:, TC - 1:TC], sl_, ALU.mult, reads=["Gre", "CS1"], writes=["tmpc"])
                            P.tt("vector", hsp[:, pr, 0:1], Gre[:, TC - 1:TC], cl, ALU.mult, reads=["Gre", "CS1"], writes=["hsp"])
                            P.tt("vector", hsp[:, pr, 0:1], hsp[:, pr, 0:1], ta_, ALU.subtract, reads=["hsp", "tmpc"], writes=["hsp"])
                            P.tt("vector", hsp[:, pr, 1:2], Gim[:, TC - 1:TC], cl, ALU.mult, reads=["Gim", "CS1"], writes=["hsp"])
                            P.tt("vector", hsp[:, pr, 1:2], hsp[:, pr, 1:2], tb_, ALU.add, reads=["hsp", "tmpc"], writes=["hsp"])
                        P.ts("vector", ta_, Gim[:, TC - 1:TC], sT[:, pr:pr + 1], None, ALU.mult, reads=["Gim", "PW"], writes=["tmpc"])
                        P.ts("vector", tb_, Gre[:, TC - 1:TC], sT[:, pr:pr + 1], None, ALU.mult, reads=["Gre", "PW"], writes=["tmpc"])
                        P.stt("vector", gi_re[:, pr:pr + 1], Gre[:, TC - 1:TC], cT[:, pr:pr + 1], ta_, ALU.mult, ALU.subtract, reads=["Gre", "PW", "tmpc"], writes=["gi"])
                        P.stt("vector", gi_im[:, pr:pr + 1], Gim[:, TC - 1:TC], cT[:, pr:pr + 1], tb_, ALU.mult, ALU.add, reads=["Gim", "PW", "tmpc"], writes=["gi"])
                        P.tt("vector", q1, Gre, c1, ALU.mult, reads=["Gre", "CS1"], writes=["q1"])
                        P.tt("vector", q2, Gim, s1, ALU.mult, reads=["Gim", "CS1"], writes=["q2"])
                        P.tt("vector", hre16, q1, q2, ALU.subtract, reads=["q1", "q2"], writes=["H16r"])
                        P.tt("vector", q1, Gim, c1, ALU.mult, reads=["Gim", "CS1", "H16r"], writes=["q1"])
                        P.tt("vector", q2, Gre, s1, ALU.mult, reads=["Gre", "CS1", "H16r"], writes=["q2"])
                        P.tt("vector", him16, q1, q2, ALU.add, reads=["q1", "q2"], writes=["H16i"])
                    else:
                        h0r, h0i = h0t[:, pr, :, 0], h0t[:, pr, :, 1]
                        hnr, hni = hsd[:, pr, :, 0], hsd[:, pr, :, 1]
                        qa, qb_ = q1[:, 0:n], q2[:, 0:n]
                        lr, li = lb_re[:, pr:pr + 1], lb_im[:, pr:pr + 1]
                        P.ts("vector", qa, h0i, li, None, ALU.mult, reads=["h0t", "lb_im"], writes=["q1"])
                        P.stt("vector", qa, h0r, lr, qa, ALU.mult, ALU.subtract, reads=["h0t", "lb_re", "q1"], writes=["q1"])
                        P.tt("vector", hnr, qa, bur, ALU.add, reads=["q1", bkey], writes=["hsd"])
                        P.ts("vector", qb_, h0r, li, None, ALU.mult, reads=["h0t", "lb_im"], writes=["q2"])
                        P.stt("vector", qb_, h0i, lr, qb_, ALU.mult, ALU.add, reads=["h0t", "lb_re", "q2"], writes=["q2"])
                        P.tt("vector", hni, qb_, bui, ALU.add, reads=["q2", bkey], writes=["hsd"])
                        P.cp("vector", hre16, hnr, reads=["hsd"], writes=["H16r"])
                        P.cp("vector", him16, hni, reads=["hsd"], writes=["H16i"])
                    P.mm(yb[:, 0:n], WC[:, 0, pr, :], hre16, start=(pm == 0), stop=False, reads=["WC", "H16r"], writes=[ykey])
                    P.mm(yb[:, 0:n], WC[:, 1, pr, :], him16, start=False, stop=(pm == 3), reads=["WC", "H16i"], writes=[ykey])
                P.stt("vector", yv[:, ct, 0:n], uT[:, ct, o0:o0 + n], dsk[:, ct:ct + 1], yb[:, 0:n], ALU.mult, ALU.add, reads=["uT", "dsk", ykey], writes=["yv"])
                P.act(yg[:, ct, 0:n], yv[:, ct, 0:n], AF.Gelu, reads=["yv"], writes=["yg"])
                P.cp("vector", yg16[:, ct, 0:n], yg[:, ct, 0:n], reads=["yg"], writes=["yg16"])
            for ct2 in range(4):
                bank, bkey = nextbank()
                for ct in range(4):
                    P.mm(bank[:, 0:n], wg16[:, ct, ct2 * 128:(ct2 + 1) * 128], yg16[:, ct, 0:n], start=(ct == 0), stop=(ct == 3), reads=["wg16", "yg16"], writes=[bkey])
                P.act(sg[:, 0:n], bank[:, 0:n], AF.Sigmoid, reads=[bkey, "bgl"], writes=["sg"], bias=bgl[:, ct2:ct2 + 1])
                P.tt("vector", y2T[:, ct2, o0:o0 + n], yg[:, ct2, 0:n], sg[:, 0:n], ALU.mult, reads=["yg", "sg"], writes=["y2T"])
        P.ld("sync", ssmp_o[:, :, :], hsp, reads=["hsp"])
        P.ld("sync", ssms_o[:, :, :, :], hsd, reads=["hsd"])
        if dbg == "S":
            P.barrier()
            A.release(y2_mark)
            dd = A.alloc([4096], F32)
            P.memset("gpsimd", dd, 0.0, writes=["dd"])
            P.cp("vector", dd[:, 0:2064], y2T[:, 0, :], reads=["y2T"], writes=["dd"])
            P.cp("vector", dd[:, 2064:2064 + 2032], y2T[:, 3, 0:2032], reads=["y2T"], writes=["dd"])
            P.ld("sync", dbg_o[:, :], dd, reads=["dd"])
        P.barrier()
        A.release(y2_mark)


    if "C" in phases:
        zT = A.alloc([8, TOWN + NDEC], BF16)
        wo16 = A.alloc([8, 1024], BF16)
        wost = A.alloc([1024], F32)
        for kt in range(8):
            P.ld("sync", wost, w_out[:, kt, :], writes=["wost"])
            P.cp("vector", wo16[:, kt, :], wost, reads=["wost"], writes=["wo16"])
        wtile_alloc()
        wbst = [A.alloc([4, 128], F32) for _ in range(2)]
        wb16 = [A.alloc([4, 128], BF16) for _ in range(2)]
        sgb = [A.alloc([512], F32) for _ in range(2)]
        tzb = [A.alloc([512], F32) for _ in range(2)]
        for d in range(8):
            wgs, kgs = load_wtile(TI[("gs", d)])
            wga, kga = load_wtile(TI[("ga", d)])
            for i, src in enumerate((w_bs, w_ba)):
                P.ld("sync", wbst[i], src[:, :, d * 128:(d + 1) * 128], writes=[("wbst", i)])
                P.cp("vector", wb16[i], wbst[i], reads=[("wbst", i)], writes=[("wb16", i)])
            for (c0, n) in chunks(TOWN, TALL):
                o0 = c0 - TOWN
                for i, (wg, kg, src, skey) in enumerate(((wgs, kgs, y2T, "y2T"), (wga, kga, attnT, "attnT"))):
                    b1, k1 = nextbank()
                    proj(wg, kg, hT, "hT", c0, n, b1, k1)
                    P.act(sgb[i][:, 0:n], b1[:, 0:n], AF.Sigmoid, reads=[k1], writes=[("sgb", i)])
                    b2, k2 = nextbank()
                    for ct in range(4):
                        P.mm(b2[:, 0:n], wb16[i][:, ct, :], src[:, ct, o0:o0 + n], start=(ct == 0), stop=(ct == 3), reads=[("wb16", i), skey], writes=[k2])
                    P.tt("vector", tzb[i][:, 0:n], b2[:, 0:n], sgb[i][:, 0:n], ALU.mult, reads=[k2, ("sgb", i)], writes=[("tzb", i)])
                P.tt("vector", zT[:, d, o0:o0 + n], tzb[0][:, 0:n], tzb[1][:, 0:n], ALU.add, reads=[("tzb", 0), ("tzb", 1)], writes=["zT"])
        xcb = [A.alloc([1024], F32) for _ in range(2)]
        x1b = [A.alloc([1024], F32) for _ in range(2)]
        for t in range(17):
            ntok = 128 if t < 16 else NDEC
            o0 = t * 128
            par = t % 2
            src = x_all[TPRI + o0:TPRI + o0 + 128, :] if t < 16 else x_dec[:, :]
            P.ld("sync", xcb[par][0:ntok, :], src, writes=[("xcb", par)])
            for hh in range(2):
                bank, bkey = nextbank()
                for kt in range(8):
                    P.mm(bank[0:ntok, 0:512], zT[:, kt, o0:o0 + ntok], wo16[:, kt, hh * 512:(hh + 1) * 512], start=(kt == 0), stop=(kt == 7),
                         reads=["zT", "wo16"], writes=[bkey])
                P.tt("vector", x1b[par][0:ntok, hh * 512:(hh + 1) * 512], bank[0:ntok, 0:512], xcb[par][0:ntok, hh * 512:(hh + 1) * 512], ALU.add,
                     reads=[bkey, ("xcb", par)], writes=[("x1b", par)])
            P.ld("sync", x1_scr[o0:o0 + ntok, :], x1b[par][0:ntok, :], reads=[("x1b", par)], writes=["x1_scr"])
        if dbg == "C":
            dd = A.alloc([4096], F32)
            P.memset("gpsimd", dd, 0.0, writes=["dd"])
            P.cp("vector", dd[:, 0:2064], zT[:, 0, :], reads=["zT"], writes=["dd"])
            P.cp("vector", dd[:, 3072:4096], x1b[1][:, :], reads=[("x1b", 1)], writes=["dd"])
            P.ld("sync", dbg_o[:, :], dd, reads=["dd"])
    P.barrier()
    A.release(base_mark)

    if "D" in phases:
        wup16 = A.alloc([8, 4096], BF16)
        wdn16 = A.alloc([32, 1024], BF16)
        wst2 = [A.alloc([1024], F32) for _ in range(2)]
        gfin = A.alloc([1024], F32)
        P.ld("sync", gfin, gfin_d[:, :], writes=["gfin"])
        li = 0
        for kt in range(8):
            for q in range(4):
                par = li % 2
                li += 1
                P.ld("sync", wst2[par], w_up[:, kt, q * 1024:(q + 1) * 1024], writes=[("wst2", par)])
                P.ts("vector", wup16[:, kt, q * 1024:(q + 1) * 1024], wst2[par], gmlp[:, kt:kt + 1], None, ALU.mult,
                     reads=[("wst2", par), "gmlp"], writes=["wup16"])
        for f in range(32):
            par = li % 2
            li += 1
            P.ld("sync", wst2[par], w_down[:, f, :], writes=[("wst2", par)])
            P.cp("vector", wdn16[:, f, :], wst2[par], reads=[("wst2", par)], writes=["wdn16"])
        frontend_alloc()
        x1k = [A.alloc([1024], F32) for _ in range(2)]
        hmT = A.alloc([8, 256], BF16)
        a16 = [A.alloc([256], BF16) for _ in range(2)]
        r32 = [A.alloc([256], F32) for _ in range(2)]
        x2 = A.alloc([1024], F32)
        yo = A.alloc([1024], F32)
        st2 = A.alloc([2], F32)
        for ch in range(9):
            dec = ch == 8
            tiles = [(ch * 256 + j * 128, 128) for j in range(2)] if not dec else [(TOWN, NDEC)]
            ncol = sum(t[1] for t in tiles)
            for j, (o0, ntok) in enumerate(tiles):
                frontend(x1_scr[o0:o0 + ntok, :], ntok, hmT, "hmT", j * 128, keep_x=x1k[j])
            for f in range(32):
                ub, ukey = nextbank()
                for kt in range(8):
                    P.mm(ub[:, 0:ncol], wup16[:, kt, f * 128:(f + 1) * 128], hmT[:, kt, 0:ncol], start=(kt == 0), stop=(kt == 7),
                         reads=["wup16", "hmT"], writes=[ukey])
                par = f % 2
                P.ts("vector", r32[par][:, 0:ncol], ub[:, 0:ncol], 0.0, None, ALU.max, reads=[ukey], writes=[("r32", par)])
                P.tt("vector", a16[par][:, 0:ncol], r32[par][:, 0:ncol], r32[par][:, 0:ncol], ALU.mult, reads=[("r32", par)], writes=[("a16", par)])
                for j, (o0, ntok) in enumerate(tiles):
                    for hh in range(2):
                        ab, akey = banks[2 * j + hh]
                        P.mm(ab[0:ntok, 0:512], a16[par][:, j * 128:j * 128 + ntok], wdn16[:, f, hh * 512:(hh + 1) * 512], start=(f == 0), stop=(f == 31),
                             reads=[("a16", par), "wdn16"], writes=[akey])
            for j, (o0, ntok) in enumerate(tiles):
                for hh in range(2):
                    ab, akey = banks[2 * j + hh]
                    P.tt("vector", x2[0:ntok, hh * 512:(hh + 1) * 512], ab[0:ntok, 0:512], x1k[j][0:ntok, hh * 512:(hh + 1) * 512], ALU.add,
                         reads=[akey, ("xk", id(x1k[j]))], writes=["x2"])
                P.memset("gpsimd", st2[0:ntok, 0:1], 0.0, writes=["st2"])
                P.act(junk[0][0:ntok, :], x2[0:ntok, :], AF.Square, reads=["x2", "st2"], writes=["junk", "st2"], scale=1.0 / 32.0, accum_out=st2[0:ntok, 0:1])
                P.act(st2[0:ntok, 1:2], st2[0:ntok, 0:1], AF.Sqrt, reads=["st2", "epscol"], writes=["st2"], bias=epscol[0:ntok, 0:1])
                P.recip(st2[0:ntok, 1:2], st2[0:ntok, 1:2], reads=["st2"], writes=["st2"])
                P.stt("vector", yo[0:ntok, :], x2[0:ntok, :], st2[0:ntok, 1:2], gfin[0:ntok, :], ALU.mult, ALU.mult, reads=["x2", "st2", "gfin"], writes=["yo"])
                if dec:
                    P.ld("sync", yd_o[:, :], yo[0:ntok, :], reads=["yo"])
                else:
                    P.ld("sync", y_o[o0:o0 + ntok, :], yo[0:ntok, :], reads=["yo"])

    P.build()
    return nc, P


def _rope_tables(core):
    half = 8
    inv_freq = np.exp(np.float32(-math.log(500000.0)) * np.arange(half, dtype=np.float32) * np.float32(2.0 / 16)).astype(np.float32)
    pos = np.concatenate([2048 * (core - 1) + np.arange(TEXT), np.full(NDEC, 8192)]).astype(np.float32)
    ang = pos[:, None] * inv_freq[None, :]
    cos = np.cos(ang).astype(np.float32)
    sin = np.sin(ang).astype(np.float32)
    C = np.ones((128, TALL), np.float32)
    S = np.zeros((128, TALL), np.float32)
    for j in range(128):
        e = j % 64
        if e < 8:
            C[j] = cos[:, e]
            S[j] = -sin[:, e]
        elif e < 16:
            C[j] = cos[:, e - 8]
            S[j] = sin[:, e - 8]
    tm = np.zeros((128, 17, 16), np.float32)
    for t in range(16):
        sl = slice(2048 + t * 128, 2048 + (t + 1) * 128)
        tm[:, t, 0:8] = cos[sl]
        tm[:, t, 8:16] = sin[sl]
    tm[0:NDEC, 16, 0:8] = cos[TEXT:TEXT + NDEC]
    tm[0:NDEC, 16, 8:16] = sin[TEXT:TEXT + NDEC]
    return C, S, tm


def _prep_shared(inp):
    w_in = np.asarray(inp["w_in"][0], np.float32)
    sh = {}
    wt = np.empty((NT, 128, 8, 128), np.float32)
    for i, cols in enumerate(TCOLS):
        wt[i] = w_in[:, cols].reshape(8, 128, 128).transpose(1, 0, 2)
    sh["w_tiles"] = wt
    wkv = np.empty((6, 128, 8, 512), np.float32)
    for kv in range(2):
        for g in range(3):
            c0 = 1536 * (kv + 1) + g * 512
            wkv[kv * 3 + g] = w_in[:, c0:c0 + 512].reshape(8, 128, 512).transpose(1, 0, 2)
    sh["w_kv"] = wkv

    def kmaj(w, kt):
        return np.ascontiguousarray(np.asarray(w, np.float32).reshape(kt, 128, -1).transpose(1, 0, 2))

    sh["w_glu"] = kmaj(inp["w_glu"][0], 4)
    sh["w_bs"] = kmaj(inp["w_branch_ssm"][0], 4)
    sh["w_ba"] = kmaj(inp["w_branch_attn"][0], 4)
    sh["w_out"] = kmaj(inp["w_out"][0], 8)
    sh["w_up"] = kmaj(inp["w_up"][0], 8)
    sh["w_down"] = kmaj(inp["w_down"][0], 32)
    sh["gm"] = np.ascontiguousarray(np.asarray(inp["norm_mix"][0], np.float32).reshape(8, 128).T)
    sh["gmlp"] = np.ascontiguousarray(np.asarray(inp["norm_mlp"][0], np.float32).reshape(8, 128).T)
    sh["gfin"] = np.ascontiguousarray(np.broadcast_to(np.asarray(inp["norm_final"], np.float32)[None, :], (128, 1024)))

    def st(a):
        return np.asarray(a, np.float32).reshape(16, 2, 64).transpose(1, 2, 0).reshape(128, 16)

    logdt = np.broadcast_to(np.asarray(inp["ssm_log_dt"][0], np.float32)[:, None], (32, 64))
    sh["ssm_small"] = np.ascontiguousarray(np.stack([st(inp["ssm_lambda_re"][0]), st(inp["ssm_lambda_im"][0]), st(logdt)], axis=-1))

    def stb(a):
        return np.ascontiguousarray(np.asarray(a, np.float32).reshape(16, 2, 64, 16).transpose(1, 2, 0, 3).reshape(128, 16, 16))

    sh["b_re"] = stb(inp["ssm_b_re"][0])
    sh["b_im"] = stb(inp["ssm_b_im"][0])

    def stc(a):
        a = np.asarray(a, np.float32)
        o = np.zeros((128, 16, 128), np.float32)
        for pr in range(16):
            pm = pr % 4
            for gl in range(2):
                o[64 * gl:64 * gl + 64, pr, 32 * pm + 16 * gl:32 * pm + 16 * gl + 16] = a[2 * pr + gl].T
        return o

    sh["wc_re"] = stc(inp["ssm_c_re"][0])
    sh["wc_im"] = stc(inp["ssm_c_im"][0])
    sh["dskip"] = np.ascontiguousarray(np.asarray(inp["ssm_d"][0], np.float32).reshape(4, 128).T)
    sh["bglu"] = np.ascontiguousarray(np.asarray(inp["b_glu"][0], np.float32).reshape(4, 128).T)
    b = np.arange(128)[:, None]
    a = np.arange(128)[None, :]
    m2 = np.zeros((128, 256), np.float32)
    m2[:, 0:128] = np.where(b <= a, 0.0, NEG)
    m2[:, 128:256] = np.where(b >= a, 0.0, NEG)
    sh["mask2"] = m2
    em = np.zeros((8, 512), np.float32)
    for h in range(8):
        em[h, h * 64:(h + 1) * 64] = 1.0
    sh["em"] = em
    bo = np.zeros((128, 128), np.float32)
    bo[0:64, 0:64] = 1.0
    bo[64:128, 64:128] = 1.0
    sh["bones"] = bo
    return sh


def _prep_core(inp, sh, c):
    m = dict(sh)
    xp = np.asarray(inp["x_prompt"][0], np.float32)
    xa = np.zeros((TPRI + TOWN, 1024), np.float32)
    n = 2048 * c + 2048
    xa[TPRI + TOWN - n:] = xp[0:n]
    m["x_all"] = xa
    m["x_dec"] = np.ascontiguousarray(np.asarray(inp["x_sample"], np.float32)[16 * c:16 * c + 16, 0, :])
    stt = np.asarray(inp["state_ssm"][0], np.float32)[16 * c:16 * c + 16]
    m["h0"] = np.ascontiguousarray(stt.reshape(16, 16, 2, 64, 2).transpose(2, 3, 1, 0, 4).reshape(128, 16, 16, 2))
    C, S, tm = _rope_tables(c)
    m["rope_c"], m["rope_s"], m["rope_tm"] = C, S, tm
    m["maskh"] = sh["mask2"][:, 128:256].copy() if c > 0 else np.full((128, 128), NEG, np.float32)
    for g, nm in enumerate(["cache_kv_g1", "cache_kv_g2", "cache_kv_g3"]):
        cc = np.asarray(inp[nm][0], np.float32)[16 * c:16 * c + 16]
        m["cache%d" % (g + 1)] = np.ascontiguousarray(cc.reshape(16, cc.shape[1], 2, 512))
    return m


def kernel(**inp):
    nc, _ = build_program("ABSCDEFG")
    sh = _prep_shared(inp)
    in_maps = [_prep_core(inp, sh, c) for c in range(NCORE)]
    res = run_bass_kernel_spmd(nc, in_maps, core_ids=list(range(NCORE)))
    r = res.results
    y_prompt = np.concatenate([r[c]["y"] for c in range(NCORE)], 0)[None]
    y_sample = np.concatenate([r[c]["y_dec"] for c in range(NCORE)], 0)[:, None, :]
    kvp = [r[NCORE - 1]["kvp%d" % (g + 1)].reshape(1, 1, -1, 2, 8, 64) for g in range(3)]
    sp = r[NCORE - 1]["ssmp"].reshape(2, 64, 16, 2).transpose(2, 0, 1, 3).reshape(1, 1, 32, 64, 2)
    kvs = [np.concatenate([r[c]["kvs%d" % (g + 1)] for c in range(NCORE)], 0).reshape(1, 128, -1, 2, 8, 64) for g in range(3)]
    ss = np.concatenate([r[c]["ssms"].reshape(2, 64, 16, 16, 2).transpose(3, 2, 0, 1, 4).reshape(16, 32, 64, 2) for c in range(NCORE)], 0)[None]
    return (y_prompt, y_sample, kvp[0], kvp[1], kvp[2], sp, kvs[0], kvs[1], kvs[2], ss)
```
